# Optimizing a Trainium2 kernel written in Bass

```python
import jax, jax.numpy as jnp
from jax import lax
import numpy as np

D_MODEL = 2048
BATCH = 2
SEQ = 4096
DEPTH = 1

EPS = 1e-6
N_HEADS = 16
QK_NOPE = 128
QK_ROPE = 64
V_HEAD = 128
Q_LORA = 512
KV_LORA = 512
MLA_WIDTH = N_HEADS * V_HEAD
ROPE_THETA = 10000.0
Q_BLOCK = 128
POOL_WINDOWS = (2, 4, 8, 16)
POOL_GROUPS = len(POOL_WINDOWS)
POOL_GROUP_DIM = 256
POOL_WIDTH = POOL_GROUPS * POOL_GROUP_DIM
IN_SPLITS = (Q_LORA, KV_LORA, QK_ROPE, MLA_WIDTH, POOL_WIDTH, POOL_WIDTH, D_MODEL, D_MODEL)
N_IN = sum(IN_SPLITS)
IN_OFFSETS = tuple(int(v) for v in np.cumsum(IN_SPLITS)[:-1])

kernel_name = "hybrid_mla_pool_gated_encoder_block"


def _rmsnorm(x, g):
    xf = x.astype(jnp.float32)
    y = xf * lax.rsqrt(jnp.mean(xf * xf, axis=-1, keepdims=True) + EPS)
    return y.astype(x.dtype) * g


def _rope_tables(positions, dtype):
    inv_freq = 1.0 / (ROPE_THETA ** (jnp.arange(0, QK_ROPE, 2, dtype=jnp.float32) / QK_ROPE))
    ang = positions.astype(jnp.float32)[..., None] * inv_freq
    return jnp.cos(ang).astype(dtype), jnp.sin(ang).astype(dtype)


def _apply_rope(x, cos, sin):
    x1, x2 = jnp.split(x, 2, axis=-1)
    return jnp.concatenate([x1 * cos - x2 * sin, x2 * cos + x1 * sin], axis=-1)


def _mla_attention(q_nope, q_rope, k_nope, k_rope, v):
    B, S, H, _ = q_nope.shape
    n_blocks = S // Q_BLOCK
    scale = (QK_NOPE + QK_ROPE) ** -0.5

    def block(i):
        start = i * Q_BLOCK
        qn = lax.dynamic_slice_in_dim(q_nope, start, Q_BLOCK, axis=1)
        qr = lax.dynamic_slice_in_dim(q_rope, start, Q_BLOCK, axis=1)
        s = (jnp.einsum('bqhn,bkhn->bhqk', qn, k_nope)
             + jnp.einsum('bqhr,bkr->bhqk', qr, k_rope))
        p = jax.nn.softmax(s.astype(jnp.float32) * scale, axis=-1).astype(v.dtype)
        return jnp.einsum('bhqk,bkhv->bqhv', p, v)

    o = lax.map(block, jnp.arange(n_blocks))
    return jnp.moveaxis(o, 0, 1).reshape(B, S, H * V_HEAD)


def _multiscale_pool(v):
    S = v.shape[1]
    vf = v.astype(jnp.float32)
    cs = jnp.concatenate([jnp.zeros_like(vf[:, :1]), jnp.cumsum(vf, axis=1)], axis=1)
    t = jnp.arange(S)[:, None]
    w = jnp.array(POOL_WINDOWS, dtype=jnp.int32)[None, :]
    lo = jnp.clip(t - w // 2, 0, S)
    hi = jnp.clip(t + w - w // 2, 0, S)
    g = jnp.arange(POOL_GROUPS)[None, :]
    window_sum = cs[:, hi, g] - cs[:, lo, g]
    count = (hi - lo).astype(jnp.float32)[None, :, :, None]
    return (window_sum / count - vf).astype(v.dtype)


def setup_inputs(seed: int = 0) -> dict:
    key = jax.random.key(seed)
    ks = jax.random.split(key, 20)
    f32 = jnp.float32

    def w(k, shape, fan_in):
        return jax.random.normal(k, shape, f32) * (fan_in ** -0.5)

    def gain(k, shape):
        return 1.0 + 0.02 * jax.random.normal(k, shape, f32)

    x = jax.random.normal(ks[0], (BATCH, SEQ, D_MODEL), f32)
    c = jax.random.normal(ks[1], (BATCH, D_MODEL), f32)
    offsets = jax.random.randint(ks[2], (BATCH, 1), 0, 1024, dtype=jnp.int32)
    positions = offsets + jnp.arange(SEQ, dtype=jnp.int32)[None, :]
    return {
        "x": x,
        "c": c,
        "positions": positions,
        "ada_w": w(ks[3], (DEPTH, D_MODEL, 3 * D_MODEL), D_MODEL),
        "ada_b": 0.02 * jax.random.normal(ks[4], (DEPTH, 3 * D_MODEL), f32),
        "norm_g": gain(ks[5], (DEPTH, D_MODEL)),
        "w_in": w(ks[6], (DEPTH, D_MODEL, N_IN), D_MODEL),
        "q_norm_g": gain(ks[7], (DEPTH, Q_LORA)),
        "w_uq": w(ks[8], (DEPTH, Q_LORA, N_HEADS * (QK_NOPE + QK_ROPE)), Q_LORA),
        "kv_norm_g": gain(ks[9], (DEPTH, KV_LORA)),
        "w_ukv": w(ks[10], (DEPTH, KV_LORA, N_HEADS * (QK_NOPE + V_HEAD)), KV_LORA),
        "w_o_mla": w(ks[11], (DEPTH, MLA_WIDTH, D_MODEL), MLA_WIDTH),
        "pool_w": w(ks[12], (DEPTH, POOL_GROUPS, POOL_GROUP_DIM, POOL_GROUP_DIM), POOL_GROUP_DIM),
        "pool_scale": 1.0 + 0.1 * jax.random.normal(ks[13], (DEPTH, POOL_WIDTH), f32),
        "w_o_pool": w(ks[14], (DEPTH, POOL_WIDTH, D_MODEL), POOL_WIDTH),
        "w_out": w(ks[15], (DEPTH, D_MODEL, D_MODEL), D_MODEL),
        "final_g": gain(ks[16], (D_MODEL,)),
    }


def reference(x, c, positions, ada_w, ada_b, norm_g, w_in, q_norm_g, w_uq, kv_norm_g,
              w_ukv, w_o_mla, pool_w, pool_scale, w_o_pool, w_out, final_g):
    B, S, D = x.shape
    cos, sin = _rope_tables(positions, x.dtype)
    c_act = jax.nn.silu(c)
    for l in range(DEPTH):
        mod = c_act @ ada_w[l] + ada_b[l]
        shift, scale, gate = jnp.split(mod, 3, axis=-1)
        h = _rmsnorm(x, norm_g[l]) * (1.0 + scale[:, None, :]) + shift[:, None, :]

        z = h @ w_in[l]
        c_q, c_kv, k_rope, g_mla, v_pool, g_pool, m_mla, m_pool = jnp.split(z, IN_OFFSETS, axis=-1)

        q = (_rmsnorm(c_q, q_norm_g[l]) @ w_uq[l]).reshape(B, S, N_HEADS, QK_NOPE + QK_ROPE)
        q_nope, q_rope = q[..., :QK_NOPE], q[..., QK_NOPE:]
        q_rope = _apply_rope(q_rope, cos[:, :, None, :], sin[:, :, None, :])
        kv = (_rmsnorm(c_kv, kv_norm_g[l]) @ w_ukv[l]).reshape(B, S, N_HEADS, QK_NOPE + V_HEAD)
        k_nope, v = kv[..., :QK_NOPE], kv[..., QK_NOPE:]
        k_rope = _apply_rope(k_rope, cos, sin)
        attn = _mla_attention(q_nope, q_rope, k_nope, k_rope, v)
        p_mla = (attn * jax.nn.silu(g_mla)) @ w_o_mla[l]

        vp = v_pool.reshape(B, S, POOL_GROUPS, POOL_GROUP_DIM)
        pooled = _multiscale_pool(vp)
        mixed = jnp.einsum('bsgc,gcd->bsgd', pooled, pool_w[l]).reshape(B, S, POOL_WIDTH)
        p_pool = (mixed * pool_scale[l] * jax.nn.silu(g_pool)) @ w_o_pool[l]

        y = jax.nn.sigmoid(m_mla) * p_mla + jax.nn.sigmoid(m_pool) * p_pool
        x = x + gate[:, None, :] * (y @ w_out[l])
    return _rmsnorm(x, final_g)
```

```python
import numpy as np
import ml_dtypes
from contextlib import ExitStack
import concourse.bass as bass
import concourse.mybir as mybir
from concourse.bass_utils import run_bass_kernel_spmd

F32 = mybir.dt.float32
BF16 = mybir.dt.bfloat16
I32 = mybir.dt.int32
AF = mybir.ActivationFunctionType
ALU = mybir.AluOpType

D = 2048
S_TOK = 4096
OWN = 1024
NH = 16
N_IN = 9280
OFF_CQ, OFF_CKV, OFF_KR, OFF_GM, OFF_VP, OFF_GP, OFF_MM, OFF_MP = 0, 512, 1024, 1088, 3136, 4160, 5184, 7232
EPS = 1e-6
SM_SCALE = float(192 ** -0.5)
PI = float(np.pi)
C1 = 6.28125
C2 = float(2 * np.pi - 6.28125)

ENGS = ("pe", "act", "dve", "pool", "sp")


class Buf:
    __slots__ = ("name", "last_w", "readers", "dsem", "dcount")

    def __init__(self, name=""):
        self.name = name
        self.last_w = None
        self.readers = []
        self.dsem = None
        self.dcount = 0


class Sched:
    def __init__(self):
        self.streams = {e: [] for e in ENGS}
        self.ndma_sems = 0

    def _collect(self, reads, writes):
        deps = set()
        for b in reads:
            if b.last_w is not None:
                deps.add(b.last_w)
        for b in writes:
            if b.last_w is not None:
                deps.add(b.last_w)
            for r in b.readers:
                deps.add(r)
        return deps

    def _update(self, tok, reads, writes):
        for b in reads:
            b.readers.append(tok)
        for b in writes:
            b.last_w = tok
            b.readers = []

    def op(self, eng, fn, reads=(), writes=()):
        idx = len(self.streams[eng])
        deps = self._collect(reads, writes)
        tok = ("e", eng, idx)
        if eng == "pe":
            deps = {d for d in deps if not (d[0] == "e" and d[1] == "pe")}
        self.streams[eng].append({"fn": fn, "deps": deps, "signal": False, "dma": None})
        self._update(tok, reads, writes)
        return tok

    def dma(self, eng, fn, reads=(), writes=(), key=None):
        if key is None:
            key = writes[0] if writes else reads[0]
        if key.dsem is None:
            key.dsem = self.ndma_sems
            self.ndma_sems += 1
        key.dcount += 1
        deps = self._collect(reads, writes)
        deps = {d for d in deps if not (d[0] == "d" and d[1] == key.dsem)}
        tok = ("d", key.dsem, 16 * key.dcount)
        self.streams[eng].append({"fn": fn, "deps": deps, "signal": False,
                                  "dma": (key.dsem, 16 * key.dcount)})
        self._update(tok, reads, writes)
        return tok

    def barrier(self):
        bs = {e: Buf("bar_" + e) for e in ENGS}
        for e in ENGS:
            fn, extra = self.bar_ops[e]
            self.op(e, fn, writes=[bs[e]] + extra)
        for e in ENGS:
            self.op(e, lambda eng: eng.nop(), reads=[bs[o] for o in ENGS if o != e])


def build(nc, sched):
    for e in ENGS:
        seen = {}
        for o in sched.streams[e]:
            best = {}
            nd = set()
            for d in o["deps"]:
                k = (d[0], d[1])
                if d[0] == "e" and d[1] != "pe":
                    nd.add(d)
                    continue
                if k not in best or d[2] > best[k][2]:
                    best[k] = d
            for k, d in best.items():
                if seen.get(k, -1) >= d[2]:
                    continue
                seen[k] = d[2]
                nd.add(d)
            o["deps"] = nd
    for e in ENGS:
        for o in sched.streams[e]:
            for d in o["deps"]:
                if d[0] == "e":
                    sched.streams[d[1]][d[2]]["signal"] = True
    cnt = {}
    for e in ENGS:
        c = 0
        for i, o in enumerate(sched.streams[e]):
            if o["signal"] and o["dma"] is None:
                c += 1
                cnt[(e, i)] = c
    with ExitStack() as es:
        esems = {e: es.enter_context(nc.semaphore("s_" + e)) for e in ENGS}
        dsems = [es.enter_context(nc.semaphore("d%d" % i)) for i in range(sched.ndma_sems)]
        block = es.enter_context(nc.Block())

        def run(ename, eng):
            waited = {}
            for i, o in enumerate(sched.streams[ename]):
                for d in sorted(o["deps"]):
                    if d[0] == "e":
                        sem, val, k = esems[d[1]], cnt[(d[1], d[2])], ("e", d[1])
                    else:
                        sem, val, k = dsems[d[1]], d[2], ("d", d[1])
                    if waited.get(k, 0) >= val:
                        continue
                    waited[k] = val
                    eng.wait_ge(sem, val)
                ins = o["fn"](eng)
                if o["dma"] is not None:
                    ins.then_inc(dsems[o["dma"][0]], 16)
                elif o["signal"]:
                    ins.then_inc(esems[ename], 1)

        @block.sync
        def _(eng):
            run("sp", eng)

        @block.tensor
        def _(eng):
            run("pe", eng)

        @block.scalar
        def _(eng):
            run("act", eng)

        @block.vector
        def _(eng):
            run("dve", eng)

        @block.gpsimd
        def _(eng):
            run("pool", eng)


class Arena:
    def __init__(self, t, nbytes):
        self.t = t
        self.cap = nbytes
        self.top = 0

    def alloc_at(self, off, shape, dt, parts=128):
        top = self.top
        self.top = off
        v = self.alloc(shape, dt, parts)
        self.top = top
        return v

    def alloc(self, shape, dt, parts=128):
        n = 1
        for s in shape:
            n *= s
        size = n * (2 if dt == BF16 else 4)
        size = (size + 63) // 64 * 64
        off = self.top
        self.top += size
        assert self.top <= self.cap, ("arena overflow", self.top, self.cap)
        v = self.t[:, off // 2:(off + n * (2 if dt == BF16 else 4)) // 2]
        if dt != BF16:
            v = v.bitcast(dt)
        if len(shape) == 2:
            v = v.rearrange("p (a b) -> p a b", a=shape[0])
        elif len(shape) == 3:
            v = v.rearrange("p (a b c) -> p a b c", a=shape[0], b=shape[1])
        if parts < 128:
            v = v[0:parts]
        return v


ARENA_BYTES = 206 * 1024


class _Stop(Exception):
    pass


def build_program(debug=None):
    nc = bass.Bass("TRN2", target_bir_lowering=False)
    try:
        _body(nc, debug)
    except _Stop:
        pass
    return nc


def _body(nc, debug):

    def din(name, shape, dt=F32):
        return nc.dram_tensor(name, shape, dt, kind="ExternalInput").ap()

    x_d = din("x_r", [S_TOK, D])
    pos_d = din("pos_r", [1, S_TOK], I32)
    c_d = din("c_col", [128, 16])
    adaw_d = din("ada_w", [D, 3 * D])
    adabc_d = din("ada_b_col", [128, 32])
    adabg_d = din("ada_b_gate", [1, D])
    ng_d = din("norm_g_col", [128, 16])
    qg_d = din("q_norm_g_col", [128, 4])
    kvg_d = din("kv_norm_g_col", [128, 4])
    psc_d = din("pool_scale_col", [128, 8])
    fg_d = din("final_g", [1, D])
    win_d = din("w_in", [D, N_IN])
    wuq_d = din("w_uq", [512, NH * 192])
    wukv_d = din("w_ukv", [512, NH * 256])
    womla_d = din("w_o_mla", [D, D])
    poolw_d = din("pool_w", [4, 256, 256])
    wopool_d = din("w_o_pool", [1024, D])
    wout_d = din("w_out", [D, D])
    ident_d = din("ident", [128, 128], BF16)
    freq_d = din("freq_col", [128, 1])
    invc_d = din("pool_invc", [4, 128, OWN])
    mask_d = din("pool_mask", [128, 16])
    out_d = nc.dram_tensor("out", [OWN, D], F32, kind="ExternalOutput").ap()

    S = Sched()
    es = ExitStack()
    arena_t = es.enter_context(nc.sbuf_tensor("arena", [128, ARENA_BYTES // 2], BF16))
    A = Arena(arena_t, ARENA_BYTES)
    PSALL = es.enter_context(nc.psum_tensor("psall", [128, 4096], F32))
    PS = [PSALL[:, i * 512:(i + 1) * 512] for i in range(8)]
    PSB = [Buf("ps%d" % i) for i in range(8)]

    def mm(out, lhsT, rhs, start, stop, reads, writes):
        S.op("pe", lambda e: e.matmul(out, lhsT, rhs, start=start, stop=stop), reads, writes)

    def act(out, in_, func, reads, writes, bias=None, scale=None, accum_out=None):
        kw = {}
        if bias is not None:
            kw["bias"] = bias
        if scale is not None:
            kw["scale"] = scale
        if accum_out is not None:
            kw["accum_out"] = accum_out
        S.op("act", lambda e: e.activation(out=out, in_=in_, func=func, **kw), reads, writes)

    def ts(eng, out, in0, s1, s2, op0, op1, reads, writes):
        if op1 is None:
            S.op(eng, lambda e: e.tensor_scalar(out=out, in0=in0, scalar1=s1, scalar2=None, op0=op0), reads, writes)
        else:
            S.op(eng, lambda e: e.tensor_scalar(out=out, in0=in0, scalar1=s1, scalar2=s2, op0=op0, op1=op1), reads, writes)

    def tt(eng, out, in0, in1, op, reads, writes):
        S.op(eng, lambda e: e.tensor_tensor(out=out, in0=in0, in1=in1, op=op), reads, writes)

    def stt(eng, out, in0, scalar, in1, op0, op1, reads, writes):
        S.op(eng, lambda e: e.scalar_tensor_tensor(out=out, in0=in0, scalar=scalar, in1=in1, op0=op0, op1=op1), reads, writes)

    def cp(eng, out, in_, reads, writes):
        if eng == "act":
            S.op(eng, lambda e: e.copy(out=out, in_=in_), reads, writes)
        else:
            S.op(eng, lambda e: e.tensor_copy(out=out, in_=in_), reads, writes)

    def dma(eng, out, in_, reads, writes, key=None):
        S.dma(eng, lambda e: e.dma_start(out=out, in_=in_), reads, writes, key=key)

    dbg_bufs = []

    def dump(name, ap, shape, dt, bufs):
        o = nc.dram_tensor("dbg_" + name, shape, dt, kind="ExternalOutput").ap()
        B = Buf()
        S.dma("sp", lambda e: e.dma_start(out=o, in_=ap), list(bufs), [B], key=B)
        dbg_bufs.append(B)

    def checkpoint(k):
        if debug == k:
            S.op("sp", lambda e: e.nop(), [], dbg_bufs)
            S.op("act", lambda e: e.nop(), dbg_bufs, [])
            build(nc, S)
            es.close()
            raise _Stop()

    ident = A.alloc([128], BF16); IDENT = Buf()
    ones_bf = A.alloc([128], BF16); ONES = Buf()
    ones_f = A.alloc([128], F32); ONESF = Buf()
    eps_col = A.alloc([1], F32); EPSB = Buf()
    c_col = A.alloc([16], F32); CCOL = Buf()
    c_act = A.alloc([16], BF16); CACT = Buf()
    ng_col = A.alloc([16], F32); NG = Buf()
    adab_col = A.alloc([32], F32); ADAB = Buf()
    qg_col = A.alloc([4], F32); QG = Buf()
    kvg_col = A.alloc([4], F32); KVG = Buf()
    psc_col = A.alloc([8], F32); PSC = Buf()
    freq_col = A.alloc([1], F32); FREQ = Buf()
    mask_t = A.alloc([16], F32); MASK = Buf()
    a_col = A.alloc([16], F32); ACOL = Buf()
    modc = A.alloc([32], F32); MODC = Buf()
    stats = A.alloc([256], F32); STATS = [Buf() for _ in range(256)]
    bscr = A.alloc([8], F32)
    stat_i = [0]

    def newstat():
        i = stat_i[0]
        stat_i[0] += 1
        assert i < 256
        return stats[:, i:i + 1], STATS[i]

    S.bar_ops = {
        "pe": (lambda e: e.matmul(PS[7][0:1, 0:1], ones_f[0:1, 0:1], ones_f[0:1, 0:1], start=True, stop=True), [PSB[7]]),
        "act": (lambda e: e.copy(out=bscr[:, 0:1], in_=bscr[:, 1:2]), [Buf()]),
        "dve": (lambda e: e.memset(bscr[:, 2:3], 0.0), [Buf()]),
        "pool": (lambda e: e.memset(bscr[:, 3:4], 0.0), [Buf()]),
        "sp": (lambda e: e.nop(), []),
    }

    dma("sp", ident, ident_d, [], [IDENT])
    dma("sp", c_col, c_d, [], [CCOL])
    dma("sp", ng_col, ng_d, [], [NG])
    dma("sp", adab_col, adabc_d, [], [ADAB])
    dma("sp", qg_col, qg_d, [], [QG])
    dma("sp", kvg_col, kvg_d, [], [KVG])
    dma("sp", psc_col, psc_d, [], [PSC])
    dma("sp", freq_col, freq_d, [], [FREQ])
    dma("sp", mask_t, mask_d, [], [MASK])
    S.op("dve", lambda e: e.memset(ones_bf, 1.0), [], [ONES])
    S.op("dve", lambda e: e.memset(ones_f, 1.0), [], [ONESF])
    S.op("dve", lambda e: e.memset(eps_col, EPS), [], [EPSB])
    S.op("dve", lambda e: e.memset(stats, 0.0), [], STATS)
    BSCR = Buf()
    S.op("dve", lambda e: e.memset(bscr, 0.0), [], [BSCR])
    S.op("act", lambda e: e.copy(out=bscr[:, 4:5], in_=bscr[:, 5:6]), [BSCR], [])
    S.op("pool", lambda e: e.memset(bscr[:, 6:7], 0.0), [BSCR], [])
    act(c_act, c_col, AF.Silu, [CCOL], [CACT])
    if debug == -1:
        dump("cact", c_act, [128, 16], BF16, [CACT])
        dump("ident", ident, [128, 128], BF16, [IDENT])
        dump("freq", freq_col, [64, 1], F32, [FREQ])
        dump("mask", mask_t, [128, 16], F32, [MASK])
    checkpoint(-1)

    ckv_off = A.top
    ckvT = A.alloc([4, S_TOK], BF16)
    CKVT = [[Buf() for _ in range(16)] for _ in range(4)]
    after_ckv = A.top
    hT_own = A.alloc([16, OWN], BF16)
    HT_OWN = [[Buf() for _ in range(4)] for _ in range(16)]
    kr_off = A.top
    krT = A.alloc([S_TOK], BF16)
    KRT = [Buf() for _ in range(16)]
    KRU = Buf()
    S.op("pool", lambda e: e.memset(krT[64:128, :], 0.0), [], [KRU])
    cqT = A.alloc([4, OWN], BF16)
    CQT = [[Buf() for _ in range(4)] for _ in range(4)]
    qcos = A.alloc([OWN], F32); qsin = A.alloc([OWN], F32)
    QCOS = [Buf() for _ in range(4)]; QSIN = [Buf() for _ in range(4)]
    hhalo = A.alloc([16, 16], BF16); HHALO = [Buf(), Buf()]
    persist_mark = A.top

    def ada_load(col0, gi, wslots, WS, gw=512):
        sl = gi % 2
        for q in range(4):
            dma("pool", wslots[sl][:, 4 * q:4 * q + 4, :],
                adaw_d[512 * q:512 * (q + 1), col0 + gi * gw:col0 + (gi + 1) * gw].rearrange("(k p) n -> p k n", p=128),
                [], [WS[sl]])

    def ada_mm(gi, row, ROW, wslots, WS, gw=512, bank=6):
        sl = gi % 2
        for k in range(16):
            mm(PS[bank][0:1, 0:gw], c_act[:, k:k + 1], wslots[sl][:, k, :], k == 0, k == 15, [CACT, WS[sl]], [PSB[bank]])
        cp("dve", row[0:1, gi * gw:(gi + 1) * gw], PS[bank][0:1, 0:gw], [PSB[bank]], [ROW])

    def ada_cols(col0, ncols, row, ROW, wslots, WS):
        for gi in range(ncols // 512):
            ada_load(col0, gi, wslots, WS)
            ada_mm(gi, row, ROW, wslots, WS)

    w1 = A.alloc([16, 1152], BF16); W1 = Buf()
    m0 = A.top
    mod_row = A.alloc([4096], F32); MODROW = Buf()
    adaw_s = [A.alloc([16, 512], BF16) for _ in range(2)]; ADAWS = [Buf(), Buf()]
    ada_cols(0, 4096, mod_row, MODROW, adaw_s, ADAWS)
    for q in range(4):
        dma("pool", w1[:, 4 * q:4 * q + 4, 0:1088], win_d[512 * q:512 * (q + 1), 0:1088].rearrange("(k p) n -> p k n", p=128), [], [W1])
        dma("pool", w1[:, 4 * q:4 * q + 4, 1088:1120], win_d[512 * q:512 * (q + 1), 1056:1088].rearrange("(k p) n -> p k n", p=128), [], [W1])
        dma("pool", w1[:, 4 * q:4 * q + 4, 1120:1152], win_d[512 * q:512 * (q + 1), 1024:1056].rearrange("(k p) n -> p k n", p=128), [], [W1])
    if debug == -2:
        dump("modrow", mod_row[0:1, :], [1, 4096], F32, [MODROW])
    checkpoint(-2)
    for j in range(32):
        mm(PS[7][:, j:j + 1], mod_row[0:1, j * 128:(j + 1) * 128], ones_f[0:1, 0:1], True, True, [MODROW, ONESF], [PSB[7]])
    if debug == -3:
        cp("dve", modc, PS[7][:, 0:32], [PSB[7]], [MODC])
        dump("modc", modc, [128, 32], F32, [MODC])
    checkpoint(-3)
    tt("dve", modc, PS[7][:, 0:32], adab_col, ALU.add, [PSB[7], ADAB], [MODC])
    if debug == -4:
        dump("modc", modc, [128, 32], F32, [MODC])
    checkpoint(-4)
    stt("dve", a_col, modc[:, 16:32], 1.0, ng_col, ALU.add, ALU.mult, [MODC, NG], [ACOL])
    shift_col = modc[:, 0:16]
    if debug == 0:
        dump("modc", modc, [128, 32], F32, [MODC])
        dump("acol", a_col, [128, 16], F32, [ACOL])
    checkpoint(0)
    S.barrier()
    A.top = m0

    GT = 256
    NG = S_TOK // GT
    NOWN = OWN // GT
    xbuf = [A.alloc([D], F32) for _ in range(3)]; XB = [Buf() for _ in range(3)]
    xn = [A.alloc([2, D], BF16) for _ in range(2)]; XN = [[Buf(), Buf()] for _ in range(2)]
    hT_oth = [A.alloc([16, GT], BF16) for _ in range(2)]; HT_OTH = [[Buf() for _ in range(16)] for _ in range(2)]
    pos_i = A.alloc([GT], I32); POSI = Buf()
    ang = A.alloc([GT], F32); ANG = Buf()
    ki = A.alloc([GT], I32); KI = Buf()
    kf = A.alloc([GT], F32); KF = Buf()
    yc = A.alloc([GT], F32); YC = Buf()
    tcos = [A.alloc([GT], F32) for _ in range(2)]; tsin = [A.alloc([GT], F32) for _ in range(2)]
    TCOS = [Buf(), Buf()]; TSIN = [Buf(), Buf()]
    rt1 = A.alloc([GT], F32, parts=64); rt2 = A.alloc([GT], F32, parts=64); RT1 = Buf(); RT2 = Buf()
    cn = [A.alloc([512], BF16) for _ in range(2)]; CN = [Buf(), Buf()]
    junk = A.alloc([512], BF16)
    rms = A.alloc([8], F32); RMS = [Buf() for _ in range(8)]
    rstd = A.alloc([8], F32); RSTD = [Buf() for _ in range(8)]
    rr = [0]

    def rmsnorm_stat(src, SRC, n, junk_ap, JW):
        i = rr[0] % 8
        rr[0] += 1
        ssq, SSQ = newstat()
        act(junk_ap, src, AF.Square, SRC, [SSQ] + JW, accum_out=ssq)
        act(rms[:, i:i + 1], ssq, AF.Sqrt, [SSQ, EPSB], [RMS[i]], bias=eps_col, scale=1.0 / n)
        S.op("dve", lambda e: e.reciprocal(out=rstd[:, i:i + 1], in_=rms[:, i:i + 1]), [RMS[i]], [RSTD[i]])
        return rstd[:, i:i + 1], RSTD[i]

    def tview(bank):
        return PS[bank].bitcast(BF16)

    def evac_affine(out, in_, scale_ap, bias_ap, reads, writes, use_act):
        if use_act:
            if bias_ap is None:
                act(out, in_, AF.Identity, reads, writes, scale=scale_ap)
            else:
                act(out, in_, AF.Identity, reads, writes, scale=scale_ap, bias=bias_ap)
        else:
            if bias_ap is None:
                ts("dve", out, in_, scale_ap, None, ALU.mult, None, reads, writes)
            else:
                ts("dve", out, in_, scale_ap, bias_ap, ALU.mult, ALU.add, reads, writes)

    xslot = [0]

    def stage_rope(g):
        own = g < NOWN
        gs = slice(g * GT, (g + 1) * GT)
        dma("sp", pos_i, pos_d[0, gs].partition_broadcast(128), [], [POSI])
        cp("dve", ang, pos_i, [POSI], [ANG])
        ts("dve", ang, ang, freq_col[:, 0:1], None, ALU.mult, None, [ANG, FREQ], [ANG])
        ts("dve", ki, ang, float(1.0 / (2 * np.pi)), 0.5, ALU.mult, ALU.add, [ANG], [KI])
        cp("dve", kf, ki, [KI], [KF])
        stt("dve", ang, kf, -C1, ang, ALU.mult, ALU.add, [KF, ANG], [ANG])
        stt("dve", ang, kf, -C2, ang, ALU.mult, ALU.add, [KF, ANG], [ANG])
        ts("dve", kf, ang, -PI, 2 * PI, ALU.is_lt, ALU.mult, [ANG], [KF])
        tt("dve", ang, ang, kf, ALU.add, [ANG, KF], [ANG])
        ts("dve", kf, ang, PI / 2, -2 * PI, ALU.is_gt, ALU.mult, [ANG], [KF])
        stt("dve", yc, ang, PI / 2, kf, ALU.add, ALU.add, [ANG, KF], [YC])
        if own:
            sin_t, cos_t, SINB, COSB = qsin[:, gs], qcos[:, gs], QSIN[g], QCOS[g]
        else:
            sin_t, cos_t, SINB, COSB = tsin[g % 2], tcos[g % 2], TSIN[g % 2], TCOS[g % 2]
        act(sin_t, ang, AF.Sin, [ANG], [SINB])
        act(cos_t, yc, AF.Sin, [YC], [COSB])

    def stage_xn(g):
        xs = g % 2
        for t in range(2):
            sl = xslot[0] % 3
            xslot[0] += 1
            row0 = g * GT + t * 128
            dma("sp", xbuf[sl], x_d[row0:row0 + 128, :], [], [XB[sl]])
            r_ap, R = rmsnorm_stat(xbuf[sl], [XB[sl]], D, xn[xs][:, t, :], [XN[xs][t]])
            ts("dve", xn[xs][:, t, :], xbuf[sl], r_ap, None, ALU.mult, None, [XB[sl], R], [XN[xs][t]])

    def hT_ap(g, k, lo, hi):
        if g < NOWN:
            return hT_own[:, k, g * GT + lo:g * GT + hi]
        return hT_oth[g % 2][:, k, lo:hi]

    def hT_buf(g, k):
        return HT_OWN[k][g] if g < NOWN else HT_OTH[g % 2][k]

    def stage_transpose(g):
        xs = g % 2
        for bi, bank in ((1, 6), (2, 7), (3, 1), (0, 0)):
            for cc in range(4):
                c = bi * 4 + cc
                for t in range(2):
                    S.op("pe", (lambda o, i: (lambda e: e.transpose(o, i, ident)))(
                        tview(bank)[:, cc * 256 + t * 128:cc * 256 + (t + 1) * 128], xn[xs][:, t, c * 128:(c + 1) * 128]),
                        [XN[xs][t], IDENT], [PSB[bank]])
            for cc in range(4):
                c = bi * 4 + cc
                evac_affine(hT_ap(g, c, 0, GT), tview(bank)[:, cc * 256:(cc + 1) * 256], a_col[:, c:c + 1], shift_col[:, c:c + 1],
                            [PSB[bank], ACOL, MODC], [hT_buf(g, c)], use_act=(bi == 0))
        if g == NOWN:
            cp("pool", hhalo[:, :, 8:16], hT_oth[g % 2][:, :, 0:8], HT_OTH[g % 2], [HHALO[1]])
        if g == NG - 1:
            cp("pool", hhalo[:, :, 0:8], hT_oth[g % 2][:, :, GT - 8:GT], HT_OTH[g % 2], [HHALO[0]])

    def stage_krope(g):
        own = g < NOWN
        gs = slice(g * GT, (g + 1) * GT)
        if own:
            sin_t, cos_t, SINB, COSB = qsin[:, gs], qcos[:, gs], QSIN[g], QCOS[g]
        else:
            sin_t, cos_t, SINB, COSB = tsin[g % 2], tcos[g % 2], TSIN[g % 2], TCOS[g % 2]
        for k in range(16):
            mm(PS[4][:, 0:GT], w1[:, k, 1024:1152], hT_ap(g, k, 0, GT), k == 0, k == 15, [W1, hT_buf(g, k)], [PSB[4]])
        tt("dve", rt1, PS[4][0:64, 0:GT], cos_t[0:64], ALU.mult, [PSB[4], COSB], [RT1])
        tt("dve", rt2, PS[4][64:128, 0:GT], sin_t[64:128], ALU.mult, [PSB[4], SINB], [RT2])
        tt("dve", krT[0:64, gs], rt1, rt2, ALU.add, [RT1, RT2], [KRT[g]])

    cnr = [0]

    def latent_mm(g, coff):
        tiles = []
        for t in range(2):
            pb = 2 + (cnr[0] % 2)
            ci = cnr[0] % 2
            cnr[0] += 1
            for k in range(16):
                mm(PS[pb], hT_ap(g, k, t * 128, (t + 1) * 128), w1[:, k, coff:coff + 512], k == 0, k == 15, [W1, hT_buf(g, k)], [PSB[pb]])
            r_ap, R = rmsnorm_stat(PS[pb], [PSB[pb]], 512, junk, [])
            ts("dve", cn[ci], PS[pb], r_ap, None, ALU.mult, None, [PSB[pb], R], [CN[ci]])
            tiles.append(ci)
        return tiles

    def latent_tr(tiles, gcol, GB, dstf, DSTF, tbank, use_act):
        for t, ci in enumerate(tiles):
            for j in range(4):
                S.op("pe", (lambda o, i: (lambda e: e.transpose(o, i, ident)))(
                    tview(tbank)[:, j * 256 + t * 128:j * 256 + (t + 1) * 128], cn[ci][:, j * 128:(j + 1) * 128]),
                    [CN[ci], IDENT], [PSB[tbank]])
        for j in range(4):
            evac_affine(dstf(j), tview(tbank)[:, j * 256:(j + 1) * 256], gcol[:, j:j + 1], None, [PSB[tbank], GB], [DSTF(j)], use_act=use_act)

    stage_rope(0)
    stage_xn(0)
    for g in range(NG):
        gs = slice(g * GT, (g + 1) * GT)
        stage_transpose(g)
        if g + 1 < NG:
            stage_rope(g + 1)
        tl = latent_mm(g, 512)
        if g + 1 < NG:
            stage_xn(g + 1)
        stage_krope(g)
        latent_tr(tl, kvg_col, KVG, (lambda j, gs=gs: ckvT[:, j, gs]), (lambda j, g=g: CKVT[j][g]), 0, True)
        if g < NOWN:
            tl = latent_mm(g, 0)
            latent_tr(tl, qg_col, QG, (lambda j, gs=gs: cqT[:, j, gs]), (lambda j, g=g: CQT[j][g]), 1, False)
    if debug == 1:
        dump("hT_own", hT_own, [128, 16, OWN], BF16, [b for r in HT_OWN for b in r])
        dump("ckvT", ckvT, [128, 4, S_TOK], BF16, [b for r in CKVT for b in r])
        dump("krT", krT[0:64, :], [64, S_TOK], BF16, KRT)
        dump("cqT", cqT, [128, 4, OWN], BF16, [b for r in CQT for b in r])
        dump("qcos", qcos[0:64, :], [64, OWN], F32, QCOS)
        dump("qsin", qsin[0:64, :], [64, OWN], F32, QSIN)
        dump("hhalo", hhalo, [128, 16, 16], BF16, HHALO)
    checkpoint(1)
    S.barrier()
    A.top = persist_mark
    HT_OWN = [[Buf() for _ in range(2)] for _ in range(16)]
    CKVT = [[Buf() for _ in range(8)] for _ in range(4)]
    KRT = [Buf() for _ in range(8)]
    CQT = [[Buf() for _ in range(2)] for _ in range(4)]
    QCOS = [Buf(), Buf()]; QSIN = [Buf(), Buf()]
    HHALO = [Buf(), Buf()]

    sg = A.alloc([16, OWN], BF16); SG = [[Buf(), Buf()] for _ in range(16)]
    gp = A.alloc([8, OWN], BF16); GP = [[Buf(), Buf()] for _ in range(8)]
    mark1b = A.top
    ws = [A.alloc([16, 128], BF16) for _ in range(4)]; WS = [Buf() for _ in range(4)]
    pw = A.alloc([4, 2, 256], BF16); PW = Buf()
    vpx = A.alloc([2, 1040], F32); VPX = [Buf(), Buf()]
    s_a = A.alloc([2, 1040], F32); SA = Buf()
    s_b = A.alloc([2, 1040], F32); SBb = Buf()
    invc = A.alloc([OWN], F32); INVC = Buf()
    pooled = A.alloc([2, OWN], BF16); POOLED = Buf()
    dma("pool", pw, poolw_d.rearrange("g (kc p) d -> p g kc d", p=128), [], [PW])
    wsi = [0]
    prr = [0]

    def load_wchunk(wd, col, nk=16):
        sl = wsi[0] % 4
        wsi[0] += 1
        dma("pool", ws[sl][:, 0:nk, :], wd[:, col:col + 128].rearrange("(k p) n -> p k n", p=128), [], [WS[sl]])
        return sl

    def proj_own(sl, tg, pb):
        for k in range(16):
            mm(PS[pb][:, :], ws[sl][:, k, :], hT_own[:, k, tg * 512:(tg + 1) * 512], k == 0, k == 15, [WS[sl], HT_OWN[k][tg]], [PSB[pb]])

    def gmla_chunk(j):
        sl = load_wchunk(win_d, OFF_GM + j * 128)
        for tg in range(2):
            pb = prr[0] % 4; prr[0] += 1
            proj_own(sl, tg, pb)
            act(sg[:, j, tg * 512:(tg + 1) * 512], PS[pb][:, :], AF.Silu, [PSB[pb]], [SG[j][tg]])

    for j in range(8):
        sl = load_wchunk(win_d, OFF_GP + j * 128)
        for tg in range(2):
            pb = prr[0] % 4; prr[0] += 1
            proj_own(sl, tg, pb)
            act(gp[:, j, tg * 512:(tg + 1) * 512], PS[pb][:, :], AF.Silu, [PSB[pb]], [GP[j][tg]])
    for gi in range(4):
        dma("sp", invc, invc_d[gi], [], [INVC])
        for cc in range(2):
            j = gi * 2 + cc
            sl = load_wchunk(win_d, OFF_VP + j * 128)
            for tg in range(2):
                pb = prr[0] % 4; prr[0] += 1
                proj_own(sl, tg, pb)
                cp("act" if tg == 0 else "dve", vpx[:, cc, 8 + tg * 512:8 + (tg + 1) * 512], PS[pb][:, :], [PSB[pb]], [VPX[cc]])
            for k in range(16):
                mm(PS[4][:, 0:16], ws[sl][:, k, :], hhalo[:, k, :], k == 0, k == 15, [WS[sl]] + HHALO, [PSB[4]])
            tt("dve", vpx[:, cc, 0:8], PS[4][:, 0:8], mask_t[:, 0:8], ALU.mult, [PSB[4], MASK], [VPX[cc]])
            tt("dve", vpx[:, cc, 1032:1040], PS[4][:, 8:16], mask_t[:, 8:16], ALU.mult, [PSB[4], MASK], [VPX[cc]])
        lv = gi + 1
        tt("dve", s_a[:, :, 1:1040], vpx[:, :, 0:1039], vpx[:, :, 1:1040], ALU.add, VPX, [SA])
        cur, CUR, oth, OTH = s_a, SA, s_b, SBb
        if lv >= 2:
            tt("dve", oth[:, :, 2:1039], cur[:, :, 1:1038], cur[:, :, 3:1040], ALU.add, [CUR], [OTH])
            cur, CUR, oth, OTH = oth, OTH, cur, CUR
        if lv >= 3:
            tt("dve", oth[:, :, 4:1037], cur[:, :, 2:1035], cur[:, :, 6:1039], ALU.add, [CUR], [OTH])
            cur, CUR, oth, OTH = oth, OTH, cur, CUR
        if lv >= 4:
            tt("dve", oth[:, :, 8:1033], cur[:, :, 4:1029], cur[:, :, 12:1037], ALU.add, [CUR], [OTH])
            cur, CUR, oth, OTH = oth, OTH, cur, CUR
        for cc in range(2):
            tt("dve", oth[:, cc, 8:1032], cur[:, cc, 8:1032], invc, ALU.mult, [CUR, INVC], [OTH])
            tt("dve", pooled[:, cc, :], oth[:, cc, 8:1032], vpx[:, cc, 8:1032], ALU.subtract, [OTH, VPX[cc]], [POOLED])
        for jm in range(4):
            gmla_chunk(gi * 4 + jm)
        for dc in range(2):
            j = gi * 2 + dc
            for tg in range(2):
                pb = 5 + (prr[0] % 2); prr[0] += 1
                for kc in range(2):
                    mm(PS[pb][:, :], pw[:, gi, kc, dc * 128:(dc + 1) * 128], pooled[:, kc, tg * 512:(tg + 1) * 512], kc == 0, kc == 1, [PW, POOLED], [PSB[pb]])
                stt("dve", gp[:, j, tg * 512:(tg + 1) * 512], PS[pb][:, :], psc_col[:, j:j + 1], gp[:, j, tg * 512:(tg + 1) * 512],
                    ALU.mult, ALU.mult, [PSB[pb], PSC, GP[j][tg]], [GP[j][tg]])
    if debug == 2:
        dump("sg", sg, [128, 16, OWN], BF16, [b for r in SG for b in r])
        dump("gp", gp, [128, 8, OWN], BF16, [b for r in GP for b in r])
    checkpoint(2)
    S.barrier()
    A.top = mark1b

    mark2 = A.top
    wkv = [A.alloc([4, 256], BF16) for _ in range(2)]; WKV = [Buf(), Buf()]
    wq = [A.alloc([4, 256], BF16) for _ in range(2)]; WQ = [Buf(), Buf()]
    KT = [A.alloc([S_TOK], BF16) for _ in range(2)]; KTB = [[Buf() for _ in range(8)] for _ in range(2)]
    VV = [A.alloc([32, 128], BF16) for _ in range(2)]; VVB = [[Buf() for _ in range(32)] for _ in range(2)]
    QN = [A.alloc([OWN], BF16) for _ in range(2)]; QNB = [[Buf(), Buf()] for _ in range(2)]
    QR = [A.alloc([OWN], BF16) for _ in range(2)]; QRB = [[Buf(), Buf()] for _ in range(2)]
    QRU = [Buf(), Buf()]
    for i_ in range(2):
        S.op("pool", (lambda i_: (lambda e: e.memset(QR[i_][64:128, :], 0.0)))(i_), [], [QRU[i_]])
    NPT = 3
    PT = [A.alloc([1024], BF16) for _ in range(NPT)]; PTB = [Buf() for _ in range(NPT)]
    s1 = [A.alloc([512], BF16) for _ in range(2)]; S1 = [Buf(), Buf()]
    ssum = [A.alloc([512], BF16) for _ in range(2)]; SSUM = [Buf(), Buf()]
    rden = A.alloc([512], F32); RDEN = Buf()
    otmp = A.alloc([512], F32); OTMP = Buf()
    q1 = A.alloc([512], F32, parts=64); q2 = A.alloc([512], F32, parts=64); Q1 = Buf(); Q2 = Buf()
    xr = [0]
    pti = [0]

    xb_banks = [list(range(8))]

    def xbank():
        b = xb_banks[0][xr[0] % len(xb_banks[0])]
        xr[0] += 1
        return b
    LA = 3

    def load_head_w(h):
        hs = h % 2
        dma("pool", wkv[hs], wukv_d[:, h * 256:(h + 1) * 256].rearrange("(k p) n -> p k n", p=128), [], [WKV[hs]])
        dma("pool", wq[hs][:, :, 0:192], wuq_d[:, h * 192:(h + 1) * 192].rearrange("(k p) n -> p k n", p=128), [], [WQ[hs]])
        dma("pool", wq[hs][:, :, 192:224], wuq_d[:, h * 192 + 160:h * 192 + 192].rearrange("(k p) n -> p k n", p=128), [], [WQ[hs]])
        dma("pool", wq[hs][:, :, 224:256], wuq_d[:, h * 192 + 128:h * 192 + 160].rearrange("(k p) n -> p k n", p=128), [], [WQ[hs]])

    def expansion_tasks(h):
        hs = h % 2
        tasks = []

        def k_task(tg):
            pb = xbank()
            for c in range(4):
                mm(PS[pb], wkv[hs][:, c, 0:128], ckvT[:, c, tg * 512:(tg + 1) * 512], c == 0, c == 3, [WKV[hs], CKVT[c][tg]], [PSB[pb]])
            cp("act" if tg % 2 == 0 else "dve", KT[hs][:, tg * 512:(tg + 1) * 512], PS[pb], [PSB[pb]], [KTB[hs][tg]])

        def v_task(j4):
            pb = xbank()
            for jj in range(4):
                j = j4 * 4 + jj
                for c in range(4):
                    mm(PS[pb][:, jj * 128:(jj + 1) * 128], ckvT[:, c, j * 128:(j + 1) * 128], wkv[hs][:, c, 128:256], c == 0, c == 3,
                       [WKV[hs], CKVT[c][j // 4]], [PSB[pb]])
            cp("dve" if j4 % 2 == 0 else "act", VV[hs][:, j4 * 4:(j4 + 1) * 4, :], PS[pb].rearrange("p (a b) -> p a b", a=4),
               [PSB[pb]], [VVB[hs][j4 * 4 + jj] for jj in range(4)])

        def qn_task(qg):
            pb = xbank()
            for c in range(4):
                mm(PS[pb], wq[hs][:, c, 0:128], cqT[:, c, qg * 512:(qg + 1) * 512], c == 0, c == 3, [WQ[hs], CQT[c][qg]], [PSB[pb]])
            cp("act", QN[hs][:, qg * 512:(qg + 1) * 512], PS[pb], [PSB[pb]], [QNB[hs][qg]])

        def qr_task(qg):
            pa = xbank()
            for c in range(4):
                mm(PS[pa], wq[hs][:, c, 128:256], cqT[:, c, qg * 512:(qg + 1) * 512], c == 0, c == 3, [WQ[hs], CQT[c][qg]], [PSB[pa]])
            tt("dve", q1, PS[pa][0:64, :], qcos[0:64, qg * 512:(qg + 1) * 512], ALU.mult, [PSB[pa], QCOS[qg]], [Q1])
            tt("dve", q2, PS[pa][64:128, :], qsin[64:128, qg * 512:(qg + 1) * 512], ALU.mult, [PSB[pa], QSIN[qg]], [Q2])
            tt("dve", QR[hs][0:64, qg * 512:(qg + 1) * 512], q1, q2, ALU.add, [Q1, Q2], [QRB[hs][qg]])

        for tg in range(8):
            tasks.append((lambda tg=tg: k_task(tg)))
        for j4 in range(8):
            tasks.append((lambda j4=j4: v_task(j4)))
        for qg in range(2):
            tasks.append((lambda qg=qg: qn_task(qg)))
            tasks.append((lambda qg=qg: qr_task(qg)))
        return tasks

    load_head_w(0)
    for tsk in expansion_tasks(0):
        tsk()
    xb_banks[0] = [6, 7]
    for h in range(NH):
        hs = h % 2
        nxt = expansion_tasks(h + 1) if h + 1 < NH else []
        if h + 1 < NH:
            load_head_w(h + 1)
        for qg in range(2):
            qs = slice(qg * 512, (qg + 1) * 512)

            def s_pair(j):
                for u in range(2):
                    kc = 2 * j + u
                    sb_ = (j % 2) * 2 + u
                    mm(PS[sb_], KT[hs][:, kc * 128:(kc + 1) * 128], QN[hs][:, qs], True, False, [KTB[hs][kc // 4], QNB[hs][qg]], [PSB[sb_]])
                    mm(PS[sb_], krT[:, kc * 128:(kc + 1) * 128], QR[hs][:, qs], False, True, [KRT[kc // 4], KRU, QRB[hs][qg], QRU[hs]], [PSB[sb_]])
                b0 = (j % 2) * 2
                p = j % NPT
                act(PT[p], PSALL[:, b0 * 512:(b0 + 2) * 512], AF.Exp, [PSB[b0], PSB[b0 + 1]], [PTB[p]], scale=SM_SCALE)

            def pv_pair(j):
                p = j % NPT
                for u in range(2):
                    kc = 2 * j + u
                    mm(PS[4], VV[hs][:, kc, :], PT[p][:, u * 512:(u + 1) * 512], kc == 0, kc == 31, [VVB[hs][kc], PTB[p]], [PSB[4]])
                tt("dve", s1[j % 2], PT[p][:, 0:512], PT[p][:, 512:1024], ALU.add, [PTB[p]], [S1[j % 2]])
                if j % 2 == 1:
                    g4 = j // 2
                    tt("dve", ssum[g4 % 2], s1[0], s1[1], ALU.add, [S1[0], S1[1]], [SSUM[g4 % 2]])

            def den_mm(g4):
                mm(PS[5], ones_bf, ssum[g4 % 2], g4 == 0, g4 == 7, [ONES, SSUM[g4 % 2]], [PSB[5]])

            issued = 0
            for j in range(16):
                while issued < min(16, j + 2):
                    s_pair(issued)
                    issued += 1
                pv_pair(j)
                if j % 2 == 0 and j >= 2:
                    den_mm(j // 2 - 1)
                if nxt and (qg * 16 + j) >= 2:
                    nxt.pop(0)()
            den_mm(7)
            if qg == 1:
                while nxt:
                    nxt.pop(0)()
            cp("dve", otmp, PS[4], [PSB[4]], [OTMP])
            S.op("dve", lambda e: e.reciprocal(out=rden, in_=PS[5]), [PSB[5]], [RDEN])
            tt("dve", otmp, otmp, rden, ALU.mult, [OTMP, RDEN], [OTMP])
            tt("dve", sg[:, h, qs], otmp, sg[:, h, qs], ALU.mult, [OTMP, SG[h][qg]], [SG[h][qg]])
    if debug == 3:
        dump("attn", sg, [128, 16, OWN], BF16, [b for r in SG for b in r])
    checkpoint(3)
    S.barrier()
    A.top = mark2

    yT = A.alloc_at(ckv_off, [16, OWN], BF16); YT = [[Buf(), Buf()] for _ in range(16)]
    mark3 = A.top
    wsA = [A.alloc([16, 128], BF16) for _ in range(2)]; WSA = [Buf(), Buf()]
    wsB = [A.alloc([8, 128], BF16) for _ in range(2)]; WSB = [Buf(), Buf()]
    wsC = [A.alloc([16, 128], BF16) for _ in range(2)]; WSC = [Buf(), Buf()]
    wsD = [A.alloc([16, 128], BF16) for _ in range(2)]; WSD = [Buf(), Buf()]
    sm1 = A.alloc([512], F32); sm2 = A.alloc([512], F32); SM1 = Buf(); SM2 = Buf()
    y1 = A.alloc([512], F32); y2 = A.alloc([512], F32); Y1 = Buf(); Y2 = Buf()
    gate_b = A.alloc_at(kr_off, [D], F32); GATEB = [Buf() for _ in range(4)]
    gate_row = A.alloc_at(kr_off + 8192, [D], F32); GROW = Buf()
    adabg_b = A.alloc_at(kr_off + 16384, [D], F32); ADABG = Buf()
    adaw_s2 = [A.alloc([16, 256], BF16) for _ in range(2)]; ADAWS2 = [Buf(), Buf()]
    dma("sp", adabg_b, adabg_d[0, :].partition_broadcast(128), [], [ADABG])
    it = 0
    for dc in range(16):
        s2 = dc % 2
        dma("pool", wsA[s2], womla_d[:, dc * 128:(dc + 1) * 128].rearrange("(k p) n -> p k n", p=128), [], [WSA[s2]])
        dma("pool", wsB[s2], wopool_d[:, dc * 128:(dc + 1) * 128].rearrange("(k p) n -> p k n", p=128), [], [WSB[s2]])
        dma("pool", wsC[s2], win_d[:, OFF_MM + dc * 128:OFF_MM + (dc + 1) * 128].rearrange("(k p) n -> p k n", p=128), [], [WSC[s2]])
        dma("pool", wsD[s2], win_d[:, OFF_MP + dc * 128:OFF_MP + (dc + 1) * 128].rearrange("(k p) n -> p k n", p=128), [], [WSD[s2]])
        if 1 <= dc <= 8:
            ada_load(4096, dc - 1, adaw_s2, ADAWS2, gw=256)
        for tg in range(2):
            bA, bB, bC, bD = (0, 1, 2, 3) if it % 2 == 0 else (4, 5, 6, 3)
            it += 1
            tsl = slice(tg * 512, (tg + 1) * 512)
            for k in range(16):
                mm(PS[bA], wsA[s2][:, k, :], sg[:, k, tsl], k == 0, k == 15, [WSA[s2], SG[k][tg]], [PSB[bA]])
            for k in range(8):
                mm(PS[bB], wsB[s2][:, k, :], gp[:, k, tsl], k == 0, k == 7, [WSB[s2], GP[k][tg]], [PSB[bB]])
            for k in range(16):
                mm(PS[bC], wsC[s2][:, k, :], hT_own[:, k, tsl], k == 0, k == 15, [WSC[s2], HT_OWN[k][tg]], [PSB[bC]])
            for k in range(16):
                mm(PS[bD], wsD[s2][:, k, :], hT_own[:, k, tsl], k == 0, k == 15, [WSD[s2], HT_OWN[k][tg]], [PSB[bD]])
            act(sm1, PS[bC], AF.Sigmoid, [PSB[bC]], [SM1])
            act(sm2, PS[bD], AF.Sigmoid, [PSB[bD]], [SM2])
            tt("dve", y1, PS[bA], sm1, ALU.mult, [PSB[bA], SM1], [Y1])
            tt("dve", y2, PS[bB], sm2, ALU.mult, [PSB[bB], SM2], [Y2])
            tt("dve", yT[:, dc, tsl], y1, y2, ALU.add, [Y1, Y2], [YT[dc][tg]])
        if 2 <= dc <= 9:
            ada_mm(dc - 2, gate_row, GROW, adaw_s2, ADAWS2, gw=256, bank=7)
    for cg in range(4):
        mm(PS[7], ones_f[0:1, 0:128], gate_row[0:1, cg * 512:(cg + 1) * 512], True, True, [ONESF, GROW], [PSB[7]])
        tt("dve", gate_b[:, cg * 512:(cg + 1) * 512], PS[7], adabg_b[:, cg * 512:(cg + 1) * 512], ALU.add, [PSB[7], ADABG], [GATEB[cg]])
    if debug == 4:
        dump("yT", yT, [128, 16, OWN], BF16, [b for r in YT for b in r])
    checkpoint(4)
    S.barrier()
    A.top = mark3

    A.top = after_ckv
    fg_b = A.alloc([D], F32); FGB = Buf()
    xsl = [A.alloc([512], F32) for _ in range(4)]; XSL = [Buf() for _ in range(4)]
    junk2 = A.alloc([D], BF16)
    rms2 = A.alloc([8], F32); RMS2 = [Buf() for _ in range(8)]
    rstd2 = A.alloc([8], F32); RSTD2 = [Buf() for _ in range(8)]
    assert A.top <= kr_off
    A.top = kr_off + 8192
    wo = A.alloc([16, D], BF16); WO = [Buf() for _ in range(4)]
    rbuf = [A.alloc([D], F32) for _ in range(8)]; RB = [[Buf() for _ in range(4)] for _ in range(8)]
    OUTB = [Buf() for _ in range(8)]
    for cg in range(4):
        for q in range(4):
            dma("pool", wo[:, 4 * q:4 * q + 4, cg * 512:(cg + 1) * 512],
                wout_d[512 * q:512 * (q + 1), cg * 512:(cg + 1) * 512].rearrange("(k p) n -> p k n", p=128),
                [WO[cg - 1]] if cg > 0 else [], [WO[cg]])
    dma("sp", fg_b, fg_d[0, :].partition_broadcast(128), [], [FGB])
    pr = 0
    xi = 0
    pend3b = []
    order = [(cg, tt_i) for cg in (0, 1) for tt_i in range(8)] + [(cg, tt_i) for tt_i in range(8) for cg in (2, 3)]
    for (cg, tt_i) in order:
        cs = slice(cg * 512, (cg + 1) * 512)
        if True:
            pb = pr % 6
            pr += 1
            xs_ = xi % 4
            xi += 1
            dma("sp", xsl[xs_], x_d[tt_i * 128:(tt_i + 1) * 128, cs], [], [XSL[xs_]])
            for k in range(16):
                mm(PS[pb], yT[:, k, tt_i * 128:(tt_i + 1) * 128], wo[:, k, cs], k == 0, k == 15, [YT[k][tt_i // 4], WO[cg]], [PSB[pb]])
            tt("dve", rbuf[tt_i][:, cs], PS[pb], gate_b[:, cs], ALU.mult, [PSB[pb], GATEB[cg]], [RB[tt_i][cg]])
            tt("dve", rbuf[tt_i][:, cs], rbuf[tt_i][:, cs], xsl[xs_], ALU.add, [RB[tt_i][cg], XSL[xs_]], [RB[tt_i][cg]])
            if cg == 3:
                ssq, SSQ = newstat()
                i8 = tt_i
                act(junk2, rbuf[tt_i], AF.Square, RB[tt_i], [SSQ], accum_out=ssq)
                act(rms2[:, i8:i8 + 1], ssq, AF.Sqrt, [SSQ, EPSB], [RMS2[i8]], bias=eps_col, scale=1.0 / D)
                def fin(tt_i=tt_i, i8=i8):
                    S.op("dve", lambda e: e.reciprocal(out=rstd2[:, i8:i8 + 1], in_=rms2[:, i8:i8 + 1]), [RMS2[i8]], [RSTD2[i8]])
                    act(rbuf[tt_i], rbuf[tt_i], AF.Identity, RB[tt_i] + [RSTD2[i8]], RB[tt_i], scale=rstd2[:, i8:i8 + 1])
                    tt("pool", rbuf[tt_i], rbuf[tt_i], fg_b, ALU.mult, RB[tt_i] + [FGB], RB[tt_i])
                    dma("pool", out_d[tt_i * 128:(tt_i + 1) * 128, :], rbuf[tt_i], RB[tt_i], [], key=OUTB[tt_i])
                if pend3b:
                    pend3b.pop(0)()
                pend3b.append(fin)
    while pend3b:
        pend3b.pop(0)()
    S.op("sp", lambda e: e.nop(), [], [b_ for r_ in RB for b_ in r_])

    build(nc, S)
    es.close()


_NC_CACHE = {}


def _prep_inputs(x, c, positions, ada_w, ada_b, norm_g, w_in, q_norm_g, w_uq, kv_norm_g,
                 w_ukv, w_o_mla, pool_w, pool_scale, w_o_pool, w_out, final_g):
    f32 = np.float32
    x = np.asarray(x, f32); c = np.asarray(c, f32); positions = np.asarray(positions, np.int32)

    def col(v, n):
        return np.ascontiguousarray(np.asarray(v, f32).reshape(n, 128).T)

    ada_b0 = np.asarray(ada_b, f32)[0]
    inv_freq = (1.0 / (10000.0 ** (np.arange(0, 64, 2, dtype=np.float32) / np.float32(64)))).astype(f32)
    freq_col = np.concatenate([-inv_freq, inv_freq, -inv_freq, inv_freq]).reshape(128, 1).astype(f32)
    shared = {
        "ada_w": np.ascontiguousarray(np.asarray(ada_w, f32)[0]),
        "ada_b_col": np.ascontiguousarray(np.concatenate([col(ada_b0[0:D], 16), col(ada_b0[D:2 * D], 16)], axis=1)),
        "ada_b_gate": np.ascontiguousarray(ada_b0[2 * D:3 * D].reshape(1, D)),
        "norm_g_col": col(np.asarray(norm_g, f32)[0], 16),
        "q_norm_g_col": col(np.asarray(q_norm_g, f32)[0], 4),
        "kv_norm_g_col": col(np.asarray(kv_norm_g, f32)[0], 4),
        "pool_scale_col": col(np.asarray(pool_scale, f32)[0], 8),
        "final_g": np.ascontiguousarray(np.asarray(final_g, f32).reshape(1, D)),
        "w_in": np.ascontiguousarray(np.asarray(w_in, f32)[0]),
        "w_uq": np.ascontiguousarray(np.asarray(w_uq, f32)[0]),
        "w_ukv": np.ascontiguousarray(np.asarray(w_ukv, f32)[0]),
        "w_o_mla": np.ascontiguousarray(np.asarray(w_o_mla, f32)[0]),
        "pool_w": np.ascontiguousarray(np.asarray(pool_w, f32)[0]),
        "w_o_pool": np.ascontiguousarray(np.asarray(w_o_pool, f32)[0]),
        "w_out": np.ascontiguousarray(np.asarray(w_out, f32)[0]),
        "ident": np.eye(128).astype(ml_dtypes.bfloat16),
        "freq_col": freq_col,
    }
    in_maps = []
    wins = (2, 4, 8, 16)
    for core in range(8):
        b, qi = core // 4, core % 4
        start = qi * OWN
        m = dict(shared)
        m["x_r"] = np.ascontiguousarray(np.roll(x[b], -start, axis=0))
        m["pos_r"] = np.ascontiguousarray(np.roll(positions[b], -start).reshape(1, S_TOK))
        m["c_col"] = col(c[b], 16)
        t = start + np.arange(OWN)
        invc = np.zeros((4, 128, OWN), f32)
        for gi, w in enumerate(wins):
            lo = np.clip(t - w // 2, 0, S_TOK)
            hi = np.clip(t + w - w // 2, 0, S_TOK)
            invc[gi] = (1.0 / (hi - lo).astype(f32))[None, :]
        m["pool_invc"] = invc
        halo_tok = np.concatenate([start - 8 + np.arange(8), start + OWN + np.arange(8)])
        valid = ((halo_tok >= 0) & (halo_tok < S_TOK)).astype(f32)
        m["pool_mask"] = np.ascontiguousarray(np.broadcast_to(valid[None, :], (128, 16))).astype(f32)
        in_maps.append(m)
    return in_maps


def kernel(x, c, positions, ada_w, ada_b, norm_g, w_in, q_norm_g, w_uq, kv_norm_g,
           w_ukv, w_o_mla, pool_w, pool_scale, w_o_pool, w_out, final_g):
    in_maps = _prep_inputs(x, c, positions, ada_w, ada_b, norm_g, w_in, q_norm_g, w_uq, kv_norm_g,
                           w_ukv, w_o_mla, pool_w, pool_scale, w_o_pool, w_out, final_g)
    if "nc" not in _NC_CACHE:
        _NC_CACHE["nc"] = build_program()
    nc = _NC_CACHE["nc"]
    res = run_bass_kernel_spmd(nc, in_maps, core_ids=list(range(8)))
    out = np.zeros((2, S_TOK, D), np.float32)
    for core in range(8):
        b, qi = core // 4, core % 4
        out[b, qi * OWN:(qi + 1) * OWN, :] = res.results[core]["out"]
    return out
```

```python
import numpy as np
import ml_dtypes
from contextlib import ExitStack
import concourse.bass as bass
import concourse.mybir as mybir
from concourse.bass_utils import run_bass_kernel_spmd

F32 = mybir.dt.float32
BF16 = mybir.dt.bfloat16
I32 = mybir.dt.int32
AF = mybir.ActivationFunctionType
ALU = mybir.AluOpType

D = 2048
S_TOK = 4096
OWN = 1024
NH = 16
N_IN = 9280
OFF_CQ, OFF_CKV, OFF_KR, OFF_GM, OFF_VP, OFF_GP, OFF_MM, OFF_MP = 0, 512, 1024, 1088, 3136, 4160, 5184, 7232
EPS = 1e-6
SM_SCALE = float(192 ** -0.5)
PI = float(np.pi)
C1 = 6.28125
C2 = float(2 * np.pi - 6.28125)

ENGS = ("pe", "act", "dve", "pool", "sp")


class Buf:
    __slots__ = ("name", "last_w", "readers", "dsem", "dcount")

    def __init__(self, name=""):
        self.name = name
        self.last_w = None
        self.readers = []
        self.dsem = None
        self.dcount = 0


class Sched:
    def __init__(self):
        self.streams = {e: [] for e in ENGS}
        self.ndma_sems = 0

    def _collect(self, reads, writes):
        deps = set()
        for b in reads:
            if b.last_w is not None:
                deps.add(b.last_w)
        for b in writes:
            if b.last_w is not None:
                deps.add(b.last_w)
            for r in b.readers:
                deps.add(r)
        return deps

    def _update(self, tok, reads, writes):
        for b in reads:
            b.readers.append(tok)
        for b in writes:
            b.last_w = tok
            b.readers = []

    def op(self, eng, fn, reads=(), writes=()):
        idx = len(self.streams[eng])
        deps = self._collect(reads, writes)
        tok = ("e", eng, idx)
        if eng == "pe":
            deps = {d for d in deps if not (d[0] == "e" and d[1] == "pe")}
        self.streams[eng].append({"fn": fn, "deps": deps, "signal": False, "dma": None})
        self._update(tok, reads, writes)
        return tok

    def dma(self, eng, fn, reads=(), writes=(), key=None):
        if key is None:
            key = writes[0] if writes else reads[0]
        if key.dsem is None:
            key.dsem = self.ndma_sems
            self.ndma_sems += 1
        key.dcount += 1
        deps = self._collect(reads, writes)
        deps = {d for d in deps if not (d[0] == "d" and d[1] == key.dsem)}
        tok = ("d", key.dsem, 16 * key.dcount)
        self.streams[eng].append({"fn": fn, "deps": deps, "signal": False,
                                  "dma": (key.dsem, 16 * key.dcount)})
        self._update(tok, reads, writes)
        return tok

    def barrier(self):
        bs = {e: Buf("bar_" + e) for e in ENGS}
        for e in ENGS:
            fn, extra = self.bar_ops[e]
            self.op(e, fn, writes=[bs[e]] + extra)
        for e in ENGS:
            self.op(e, lambda eng: eng.nop(), reads=[bs[o] for o in ENGS if o != e])


def build(nc, sched):
    for e in ENGS:
        seen = {}
        for o in sched.streams[e]:
            best = {}
            nd = set()
            for d in o["deps"]:
                k = (d[0], d[1])
                if d[0] == "e" and d[1] != "pe":
                    nd.add(d)
                    continue
                if k not in best or d[2] > best[k][2]:
                    best[k] = d
            for k, d in best.items():
                if seen.get(k, -1) >= d[2]:
                    continue
                seen[k] = d[2]
                nd.add(d)
            o["deps"] = nd
    for e in ENGS:
        for o in sched.streams[e]:
            for d in o["deps"]:
                if d[0] == "e":
                    sched.streams[d[1]][d[2]]["signal"] = True
    cnt = {}
    for e in ENGS:
        c = 0
        for i, o in enumerate(sched.streams[e]):
            if o["signal"] and o["dma"] is None:
                c += 1
                cnt[(e, i)] = c
    with ExitStack() as es:
        esems = {e: es.enter_context(nc.semaphore("s_" + e)) for e in ENGS}
        dsems = [es.enter_context(nc.semaphore("d%d" % i)) for i in range(sched.ndma_sems)]
        block = es.enter_context(nc.Block())

        def run(ename, eng):
            waited = {}
            for i, o in enumerate(sched.streams[ename]):
                for d in sorted(o["deps"]):
                    if d[0] == "e":
                        sem, val, k = esems[d[1]], cnt[(d[1], d[2])], ("e", d[1])
                    else:
                        sem, val, k = dsems[d[1]], d[2], ("d", d[1])
                    if waited.get(k, 0) >= val:
                        continue
                    waited[k] = val
                    eng.wait_ge(sem, val)
                ins = o["fn"](eng)
                if o["dma"] is not None:
                    ins.then_inc(dsems[o["dma"][0]], 16)
                elif o["signal"]:
                    ins.then_inc(esems[ename], 1)

        @block.sync
        def _(eng):
            run("sp", eng)

        @block.tensor
        def _(eng):
            run("pe", eng)

        @block.scalar
        def _(eng):
            run("act", eng)

        @block.vector
        def _(eng):
            run("dve", eng)

        @block.gpsimd
        def _(eng):
            run("pool", eng)


class Arena:
    def __init__(self, t, nbytes):
        self.t = t
        self.cap = nbytes
        self.top = 0

    def alloc_at(self, off, shape, dt, parts=128):
        top = self.top
        self.top = off
        v = self.alloc(shape, dt, parts)
        self.top = top
        return v

    def alloc(self, shape, dt, parts=128):
        n = 1
        for s in shape:
            n *= s
        size = n * (2 if dt == BF16 else 4)
        size = (size + 63) // 64 * 64
        off = self.top
        self.top += size
        assert self.top <= self.cap, ("arena overflow", self.top, self.cap)
        v = self.t[:, off // 2:(off + n * (2 if dt == BF16 else 4)) // 2]
        if dt != BF16:
            v = v.bitcast(dt)
        if len(shape) == 2:
            v = v.rearrange("p (a b) -> p a b", a=shape[0])
        elif len(shape) == 3:
            v = v.rearrange("p (a b c) -> p a b c", a=shape[0], b=shape[1])
        if parts < 128:
            v = v[0:parts]
        return v


ARENA_BYTES = 206 * 1024


class _Stop(Exception):
    pass


def build_program(debug=None):
    nc = bass.Bass("TRN2", target_bir_lowering=False)
    try:
        _body(nc, debug)
    except _Stop:
        pass
    return nc


def _body(nc, debug):

    def din(name, shape, dt=F32):
        return nc.dram_tensor(name, shape, dt, kind="ExternalInput").ap()

    x_d = din("x_r", [S_TOK, D])
    pos_d = din("pos_r", [1, S_TOK], I32)
    c_d = din("c_col", [128, 16])
    adaw_d = din("ada_w", [D, 3 * D])
    adabc_d = din("ada_b_col", [128, 32])
    adabg_d = din("ada_b_gate", [1, D])
    ng_d = din("norm_g_col", [128, 16])
    qg_d = din("q_norm_g_col", [128, 4])
    kvg_d = din("kv_norm_g_col", [128, 4])
    psc_d = din("pool_scale_col", [128, 8])
    fg_d = din("final_g", [1, D])
    win_d = din("w_in", [D, N_IN])
    wuq_d = din("w_uq", [512, NH * 192])
    wukv_d = din("w_ukv", [512, NH * 256])
    womla_d = din("w_o_mla", [D, D])
    poolw_d = din("pool_w", [4, 256, 256])
    wopool_d = din("w_o_pool", [1024, D])
    wout_d = din("w_out", [D, D])
    ident_d = din("ident", [128, 128], BF16)
    freq_d = din("freq_col", [128, 1])
    invc_d = din("pool_invc", [4, 128, OWN])
    mask_d = din("pool_mask", [128, 16])
    out_d = nc.dram_tensor("out", [OWN, D], F32, kind="ExternalOutput").ap()

    S = Sched()
    es = ExitStack()
    arena_t = es.enter_context(nc.sbuf_tensor("arena", [128, ARENA_BYTES // 2], BF16))
    A = Arena(arena_t, ARENA_BYTES)
    PSALL = es.enter_context(nc.psum_tensor("psall", [128, 4096], F32))
    PS = [PSALL[:, i * 512:(i + 1) * 512] for i in range(8)]
    PSB = [Buf("ps%d" % i) for i in range(8)]

    def mm(out, lhsT, rhs, start, stop, reads, writes):
        S.op("pe", lambda e: e.matmul(out, lhsT, rhs, start=start, stop=stop), reads, writes)

    def act(out, in_, func, reads, writes, bias=None, scale=None, accum_out=None):
        kw = {}
        if bias is not None:
            kw["bias"] = bias
        if scale is not None:
            kw["scale"] = scale
        if accum_out is not None:
            kw["accum_out"] = accum_out
        S.op("act", lambda e: e.activation(out=out, in_=in_, func=func, **kw), reads, writes)

    def ts(eng, out, in0, s1, s2, op0, op1, reads, writes):
        if op1 is None:
            S.op(eng, lambda e: e.tensor_scalar(out=out, in0=in0, scalar1=s1, scalar2=None, op0=op0), reads, writes)
        else:
            S.op(eng, lambda e: e.tensor_scalar(out=out, in0=in0, scalar1=s1, scalar2=s2, op0=op0, op1=op1), reads, writes)

    def tt(eng, out, in0, in1, op, reads, writes):
        S.op(eng, lambda e: e.tensor_tensor(out=out, in0=in0, in1=in1, op=op), reads, writes)

    def stt(eng, out, in0, scalar, in1, op0, op1, reads, writes):
        S.op(eng, lambda e: e.scalar_tensor_tensor(out=out, in0=in0, scalar=scalar, in1=in1, op0=op0, op1=op1), reads, writes)

    def cp(eng, out, in_, reads, writes):
        if eng == "act":
            S.op(eng, lambda e: e.copy(out=out, in_=in_), reads, writes)
        else:
            S.op(eng, lambda e: e.tensor_copy(out=out, in_=in_), reads, writes)

    def dma(eng, out, in_, reads, writes, key=None):
        S.dma(eng, lambda e: e.dma_start(out=out, in_=in_), reads, writes, key=key)

    dbg_bufs = []

    def dump(name, ap, shape, dt, bufs):
        o = nc.dram_tensor("dbg_" + name, shape, dt, kind="ExternalOutput").ap()
        B = Buf()
        S.dma("sp", lambda e: e.dma_start(out=o, in_=ap), list(bufs), [B], key=B)
        dbg_bufs.append(B)

    def checkpoint(k):
        if debug == k:
            S.op("sp", lambda e: e.nop(), [], dbg_bufs)
            S.op("act", lambda e: e.nop(), dbg_bufs, [])
            build(nc, S)
            es.close()
            raise _Stop()

    ident = A.alloc([128], BF16); IDENT = Buf()
    ones_bf = A.alloc([128], BF16); ONES = Buf()
    ones_f = A.alloc([128], F32); ONESF = Buf()
    eps_col = A.alloc([1], F32); EPSB = Buf()
    c_col = A.alloc([16], F32); CCOL = Buf()
    c_act = A.alloc([16], BF16); CACT = Buf()
    ng_col = A.alloc([16], F32); NG = Buf()
    adab_col = A.alloc([32], F32); ADAB = Buf()
    qg_col = A.alloc([4], F32); QG = Buf()
    kvg_col = A.alloc([4], F32); KVG = Buf()
    psc_col = A.alloc([8], F32); PSC = Buf()
    freq_col = A.alloc([1], F32); FREQ = Buf()
    mask_t = A.alloc([16], F32); MASK = Buf()
    a_col = A.alloc([16], F32); ACOL = Buf()
    modc = A.alloc([32], F32); MODC = Buf()
    stats = A.alloc([256], F32); STATS = [Buf() for _ in range(256)]
    bscr = A.alloc([8], F32)
    stat_i = [0]

    def newstat():
        i = stat_i[0]
        stat_i[0] += 1
        assert i < 256
        return stats[:, i:i + 1], STATS[i]

    S.bar_ops = {
        "pe": (lambda e: e.matmul(PS[7][0:1, 0:1], ones_f[0:1, 0:1], ones_f[0:1, 0:1], start=True, stop=True), [PSB[7]]),
        "act": (lambda e: e.copy(out=bscr[:, 0:1], in_=bscr[:, 1:2]), [Buf()]),
        "dve": (lambda e: e.memset(bscr[:, 2:3], 0.0), [Buf()]),
        "pool": (lambda e: e.memset(bscr[:, 3:4], 0.0), [Buf()]),
        "sp": (lambda e: e.nop(), []),
    }

    dma("sp", ident, ident_d, [], [IDENT])
    dma("sp", c_col, c_d, [], [CCOL])
    dma("sp", ng_col, ng_d, [], [NG])
    dma("sp", adab_col, adabc_d, [], [ADAB])
    dma("sp", qg_col, qg_d, [], [QG])
    dma("sp", kvg_col, kvg_d, [], [KVG])
    dma("sp", psc_col, psc_d, [], [PSC])
    dma("sp", freq_col, freq_d, [], [FREQ])
    dma("sp", mask_t, mask_d, [], [MASK])
    S.op("dve", lambda e: e.memset(ones_bf, 1.0), [], [ONES])
    S.op("dve", lambda e: e.memset(ones_f, 1.0), [], [ONESF])
    S.op("dve", lambda e: e.memset(eps_col, EPS), [], [EPSB])
    S.op("dve", lambda e: e.memset(stats, 0.0), [], STATS)
    BSCR = Buf()
    S.op("dve", lambda e: e.memset(bscr, 0.0), [], [BSCR])
    S.op("act", lambda e: e.copy(out=bscr[:, 4:5], in_=bscr[:, 5:6]), [BSCR], [])
    S.op("pool", lambda e: e.memset(bscr[:, 6:7], 0.0), [BSCR], [])
    act(c_act, c_col, AF.Silu, [CCOL], [CACT])
    if debug == -1:
        dump("cact", c_act, [128, 16], BF16, [CACT])
        dump("ident", ident, [128, 128], BF16, [IDENT])
        dump("freq", freq_col, [64, 1], F32, [FREQ])
        dump("mask", mask_t, [128, 16], F32, [MASK])
    checkpoint(-1)

    ckv_off = A.top
    ckvT = A.alloc([4, S_TOK], BF16)
    CKVT = [[Buf() for _ in range(16)] for _ in range(4)]
    after_ckv = A.top
    hT_own = A.alloc([16, OWN], BF16)
    HT_OWN = [[Buf() for _ in range(4)] for _ in range(16)]
    kr_off = A.top
    krT = A.alloc([S_TOK], BF16)
    KRT = [Buf() for _ in range(16)]
    KRU = Buf()
    S.op("pool", lambda e: e.memset(krT[64:128, :], 0.0), [], [KRU])
    cqT = A.alloc([4, OWN], BF16)
    CQT = [[Buf() for _ in range(4)] for _ in range(4)]
    qcos = A.alloc([OWN], F32); qsin = A.alloc([OWN], F32)
    QCOS = [Buf() for _ in range(4)]; QSIN = [Buf() for _ in range(4)]
    hhalo = A.alloc([16, 16], BF16); HHALO = [Buf(), Buf()]
    persist_mark = A.top

    def ada_load(col0, gi, wslots, WS, gw=512):
        sl = gi % 2
        for q in range(4):
            dma("pool", wslots[sl][:, 4 * q:4 * q + 4, :],
                adaw_d[512 * q:512 * (q + 1), col0 + gi * gw:col0 + (gi + 1) * gw].rearrange("(k p) n -> p k n", p=128),
                [], [WS[sl]])

    def ada_mm(gi, row, ROW, wslots, WS, gw=512, bank=6):
        sl = gi % 2
        for k in range(16):
            mm(PS[bank][0:1, 0:gw], c_act[:, k:k + 1], wslots[sl][:, k, :], k == 0, k == 15, [CACT, WS[sl]], [PSB[bank]])
        cp("dve", row[0:1, gi * gw:(gi + 1) * gw], PS[bank][0:1, 0:gw], [PSB[bank]], [ROW])

    def ada_cols(col0, ncols, row, ROW, wslots, WS):
        for gi in range(ncols // 512):
            ada_load(col0, gi, wslots, WS)
            ada_mm(gi, row, ROW, wslots, WS)

    w1 = A.alloc([16, 1152], BF16); W1 = Buf()
    m0 = A.top
    mod_row = A.alloc([4096], F32); MODROW = Buf()
    adaw_s = [A.alloc([16, 512], BF16) for _ in range(2)]; ADAWS = [Buf(), Buf()]
    ada_cols(0, 4096, mod_row, MODROW, adaw_s, ADAWS)
    for q in range(4):
        dma("pool", w1[:, 4 * q:4 * q + 4, 0:1088], win_d[512 * q:512 * (q + 1), 0:1088].rearrange("(k p) n -> p k n", p=128), [], [W1])
        dma("pool", w1[:, 4 * q:4 * q + 4, 1088:1120], win_d[512 * q:512 * (q + 1), 1056:1088].rearrange("(k p) n -> p k n", p=128), [], [W1])
        dma("pool", w1[:, 4 * q:4 * q + 4, 1120:1152], win_d[512 * q:512 * (q + 1), 1024:1056].rearrange("(k p) n -> p k n", p=128), [], [W1])
    if debug == -2:
        dump("modrow", mod_row[0:1, :], [1, 4096], F32, [MODROW])
    checkpoint(-2)
    for j in range(32):
        mm(PS[7][:, j:j + 1], mod_row[0:1, j * 128:(j + 1) * 128], ones_f[0:1, 0:1], True, True, [MODROW, ONESF], [PSB[7]])
    if debug == -3:
        cp("dve", modc, PS[7][:, 0:32], [PSB[7]], [MODC])
        dump("modc", modc, [128, 32], F32, [MODC])
    checkpoint(-3)
    tt("dve", modc, PS[7][:, 0:32], adab_col, ALU.add, [PSB[7], ADAB], [MODC])
    if debug == -4:
        dump("modc", modc, [128, 32], F32, [MODC])
    checkpoint(-4)
    stt("dve", a_col, modc[:, 16:32], 1.0, ng_col, ALU.add, ALU.mult, [MODC, NG], [ACOL])
    shift_col = modc[:, 0:16]
    if debug == 0:
        dump("modc", modc, [128, 32], F32, [MODC])
        dump("acol", a_col, [128, 16], F32, [ACOL])
    checkpoint(0)
    S.barrier()
    A.top = m0

    GT = 256
    NG = S_TOK // GT
    NOWN = OWN // GT
    xbuf = [A.alloc([D], F32) for _ in range(3)]; XB = [Buf() for _ in range(3)]
    xn = [A.alloc([2, D], BF16) for _ in range(2)]; XN = [[Buf(), Buf()] for _ in range(2)]
    hT_oth = [A.alloc([16, GT], BF16) for _ in range(2)]; HT_OTH = [[Buf() for _ in range(16)] for _ in range(2)]
    pos_i = A.alloc([GT], I32); POSI = Buf()
    ang = A.alloc([GT], F32); ANG = Buf()
    ki = A.alloc([GT], I32); KI = Buf()
    kf = A.alloc([GT], F32); KF = Buf()
    yc = A.alloc([GT], F32); YC = Buf()
    tcos = [A.alloc([GT], F32) for _ in range(2)]; tsin = [A.alloc([GT], F32) for _ in range(2)]
    TCOS = [Buf(), Buf()]; TSIN = [Buf(), Buf()]
    rt1 = A.alloc([GT], F32, parts=64); rt2 = A.alloc([GT], F32, parts=64); RT1 = Buf(); RT2 = Buf()
    cn = [A.alloc([512], BF16) for _ in range(2)]; CN = [Buf(), Buf()]
    junk = A.alloc([512], BF16)
    rms = A.alloc([8], F32); RMS = [Buf() for _ in range(8)]
    rstd = A.alloc([8], F32); RSTD = [Buf() for _ in range(8)]
    rr = [0]

    def rmsnorm_stat(src, SRC, n, junk_ap, JW):
        i = rr[0] % 8
        rr[0] += 1
        ssq, SSQ = newstat()
        act(junk_ap, src, AF.Square, SRC, [SSQ] + JW, accum_out=ssq)
        act(rms[:, i:i + 1], ssq, AF.Sqrt, [SSQ, EPSB], [RMS[i]], bias=eps_col, scale=1.0 / n)
        S.op("dve", lambda e: e.reciprocal(out=rstd[:, i:i + 1], in_=rms[:, i:i + 1]), [RMS[i]], [RSTD[i]])
        return rstd[:, i:i + 1], RSTD[i]

    def tview(bank):
        return PS[bank].bitcast(BF16)

    def evac_affine(out, in_, scale_ap, bias_ap, reads, writes, use_act):
        if use_act:
            if bias_ap is None:
                act(out, in_, AF.Identity, reads, writes, scale=scale_ap)
            else:
                act(out, in_, AF.Identity, reads, writes, scale=scale_ap, bias=bias_ap)
        else:
            if bias_ap is None:
                ts("dve", out, in_, scale_ap, None, ALU.mult, None, reads, writes)
            else:
                ts("dve", out, in_, scale_ap, bias_ap, ALU.mult, ALU.add, reads, writes)

    xslot = [0]

    def stage_rope(g):
        own = g < NOWN
        gs = slice(g * GT, (g + 1) * GT)
        dma("sp", pos_i, pos_d[0, gs].partition_broadcast(128), [], [POSI])
        cp("dve", ang, pos_i, [POSI], [ANG])
        ts("dve", ang, ang, freq_col[:, 0:1], None, ALU.mult, None, [ANG, FREQ], [ANG])
        ts("dve", ki, ang, float(1.0 / (2 * np.pi)), 0.5, ALU.mult, ALU.add, [ANG], [KI])
        cp("dve", kf, ki, [KI], [KF])
        stt("dve", ang, kf, -C1, ang, ALU.mult, ALU.add, [KF, ANG], [ANG])
        stt("dve", ang, kf, -C2, ang, ALU.mult, ALU.add, [KF, ANG], [ANG])
        ts("dve", kf, ang, -PI, 2 * PI, ALU.is_lt, ALU.mult, [ANG], [KF])
        tt("dve", ang, ang, kf, ALU.add, [ANG, KF], [ANG])
        ts("dve", kf, ang, PI / 2, -2 * PI, ALU.is_gt, ALU.mult, [ANG], [KF])
        stt("dve", yc, ang, PI / 2, kf, ALU.add, ALU.add, [ANG, KF], [YC])
        if own:
            sin_t, cos_t, SINB, COSB = qsin[:, gs], qcos[:, gs], QSIN[g], QCOS[g]
        else:
            sin_t, cos_t, SINB, COSB = tsin[g % 2], tcos[g % 2], TSIN[g % 2], TCOS[g % 2]
        act(sin_t, ang, AF.Sin, [ANG], [SINB])
        act(cos_t, yc, AF.Sin, [YC], [COSB])

    def stage_xn(g):
        xs = g % 2
        for t in range(2):
            sl = xslot[0] % 3
            xslot[0] += 1
            row0 = g * GT + t * 128
            dma("sp", xbuf[sl], x_d[row0:row0 + 128, :], [], [XB[sl]])
            r_ap, R = rmsnorm_stat(xbuf[sl], [XB[sl]], D, xn[xs][:, t, :], [XN[xs][t]])
            ts("dve", xn[xs][:, t, :], xbuf[sl], r_ap, None, ALU.mult, None, [XB[sl], R], [XN[xs][t]])

    def hT_ap(g, k, lo, hi):
        if g < NOWN:
            return hT_own[:, k, g * GT + lo:g * GT + hi]
        return hT_oth[g % 2][:, k, lo:hi]

    def hT_buf(g, k):
        return HT_OWN[k][g] if g < NOWN else HT_OTH[g % 2][k]

    def stage_transpose(g):
        xs = g % 2
        for bi, bank in ((1, 6), (2, 7), (3, 1), (0, 0)):
            for cc in range(4):
                c = bi * 4 + cc
                for t in range(2):
                    S.op("pe", (lambda o, i: (lambda e: e.transpose(o, i, ident)))(
                        tview(bank)[:, cc * 256 + t * 128:cc * 256 + (t + 1) * 128], xn[xs][:, t, c * 128:(c + 1) * 128]),
                        [XN[xs][t], IDENT], [PSB[bank]])
            for cc in range(4):
                c = bi * 4 + cc
                evac_affine(hT_ap(g, c, 0, GT), tview(bank)[:, cc * 256:(cc + 1) * 256], a_col[:, c:c + 1], shift_col[:, c:c + 1],
                            [PSB[bank], ACOL, MODC], [hT_buf(g, c)], use_act=(bi == 0))
        if g == NOWN:
            cp("pool", hhalo[:, :, 8:16], hT_oth[g % 2][:, :, 0:8], HT_OTH[g % 2], [HHALO[1]])
        if g == NG - 1:
            cp("pool", hhalo[:, :, 0:8], hT_oth[g % 2][:, :, GT - 8:GT], HT_OTH[g % 2], [HHALO[0]])

    def stage_krope(g):
        own = g < NOWN
        gs = slice(g * GT, (g + 1) * GT)
        if own:
            sin_t, cos_t, SINB, COSB = qsin[:, gs], qcos[:, gs], QSIN[g], QCOS[g]
        else:
            sin_t, cos_t, SINB, COSB = tsin[g % 2], tcos[g % 2], TSIN[g % 2], TCOS[g % 2]
        for k in range(16):
            mm(PS[4][:, 0:GT], w1[:, k, 1024:1152], hT_ap(g, k, 0, GT), k == 0, k == 15, [W1, hT_buf(g, k)], [PSB[4]])
        tt("dve", rt1, PS[4][0:64, 0:GT], cos_t[0:64], ALU.mult, [PSB[4], COSB], [RT1])
        tt("dve", rt2, PS[4][64:128, 0:GT], sin_t[64:128], ALU.mult, [PSB[4], SINB], [RT2])
        tt("dve", krT[0:64, gs], rt1, rt2, ALU.add, [RT1, RT2], [KRT[g]])

    cnr = [0]

    def latent_mm(g, coff):
        tiles = []
        for t in range(2):
            pb = 2 + (cnr[0] % 2)
            ci = cnr[0] % 2
            cnr[0] += 1
            for k in range(16):
                mm(PS[pb], hT_ap(g, k, t * 128, (t + 1) * 128), w1[:, k, coff:coff + 512], k == 0, k == 15, [W1, hT_buf(g, k)], [PSB[pb]])
            r_ap, R = rmsnorm_stat(PS[pb], [PSB[pb]], 512, junk, [])
            ts("dve", cn[ci], PS[pb], r_ap, None, ALU.mult, None, [PSB[pb], R], [CN[ci]])
            tiles.append(ci)
        return tiles

    def latent_tr(tiles, gcol, GB, dstf, DSTF, tbank, use_act):
        for t, ci in enumerate(tiles):
            for j in range(4):
                S.op("pe", (lambda o, i: (lambda e: e.transpose(o, i, ident)))(
                    tview(tbank)[:, j * 256 + t * 128:j * 256 + (t + 1) * 128], cn[ci][:, j * 128:(j + 1) * 128]),
                    [CN[ci], IDENT], [PSB[tbank]])
        for j in range(4):
            evac_affine(dstf(j), tview(tbank)[:, j * 256:(j + 1) * 256], gcol[:, j:j + 1], None, [PSB[tbank], GB], [DSTF(j)], use_act=use_act)

    stage_rope(0)
    stage_xn(0)
    for g in range(NG):
        gs = slice(g * GT, (g + 1) * GT)
        stage_transpose(g)
        if g + 1 < NG:
            stage_rope(g + 1)
        tl = latent_mm(g, 512)
        if g + 1 < NG:
            stage_xn(g + 1)
        stage_krope(g)
        latent_tr(tl, kvg_col, KVG, (lambda j, gs=gs: ckvT[:, j, gs]), (lambda j, g=g: CKVT[j][g]), 0, True)
        if g < NOWN:
            tl = latent_mm(g, 0)
            latent_tr(tl, qg_col, QG, (lambda j, gs=gs: cqT[:, j, gs]), (lambda j, g=g: CQT[j][g]), 1, False)
    if debug == 1:
        dump("hT_own", hT_own, [128, 16, OWN], BF16, [b for r in HT_OWN for b in r])
        dump("ckvT", ckvT, [128, 4, S_TOK], BF16, [b for r in CKVT for b in r])
        dump("krT", krT[0:64, :], [64, S_TOK], BF16, KRT)
        dump("cqT", cqT, [128, 4, OWN], BF16, [b for r in CQT for b in r])
        dump("qcos", qcos[0:64, :], [64, OWN], F32, QCOS)
        dump("qsin", qsin[0:64, :], [64, OWN], F32, QSIN)
        dump("hhalo", hhalo, [128, 16, 16], BF16, HHALO)
    checkpoint(1)
    S.barrier()
    A.top = persist_mark
    HT_OWN = [[Buf() for _ in range(2)] for _ in range(16)]
    CKVT = [[Buf() for _ in range(8)] for _ in range(4)]
    KRT = [Buf() for _ in range(8)]
    CQT = [[Buf() for _ in range(2)] for _ in range(4)]
    QCOS = [Buf(), Buf()]; QSIN = [Buf(), Buf()]
    HHALO = [Buf(), Buf()]

    sg = A.alloc([16, OWN], BF16); SG = [[Buf(), Buf()] for _ in range(16)]
    gp = A.alloc([8, OWN], BF16); GP = [[Buf(), Buf()] for _ in range(8)]
    mark1b = A.top
    ws = [A.alloc([16, 128], BF16) for _ in range(4)]; WS = [Buf() for _ in range(4)]
    pw = A.alloc([4, 2, 256], BF16); PW = Buf()
    vpx = A.alloc([2, 1040], F32); VPX = [Buf(), Buf()]
    s_a = A.alloc([2, 1040], F32); SA = Buf()
    s_b = A.alloc([2, 1040], F32); SBb = Buf()
    invc = A.alloc([OWN], F32); INVC = Buf()
    pooled = A.alloc([2, OWN], BF16); POOLED = Buf()
    dma("pool", pw, poolw_d.rearrange("g (kc p) d -> p g kc d", p=128), [], [PW])
    wsi = [0]
    prr = [0]

    def load_wchunk(wd, col, nk=16):
        sl = wsi[0] % 4
        wsi[0] += 1
        dma("pool", ws[sl][:, 0:nk, :], wd[:, col:col + 128].rearrange("(k p) n -> p k n", p=128), [], [WS[sl]])
        return sl

    def proj_own(sl, tg, pb):
        for k in range(16):
            mm(PS[pb][:, :], ws[sl][:, k, :], hT_own[:, k, tg * 512:(tg + 1) * 512], k == 0, k == 15, [WS[sl], HT_OWN[k][tg]], [PSB[pb]])

    def gmla_chunk(j):
        sl = load_wchunk(win_d, OFF_GM + j * 128)
        for tg in range(2):
            pb = prr[0] % 4; prr[0] += 1
            proj_own(sl, tg, pb)
            act(sg[:, j, tg * 512:(tg + 1) * 512], PS[pb][:, :], AF.Silu, [PSB[pb]], [SG[j][tg]])

    for j in range(8):
        sl = load_wchunk(win_d, OFF_GP + j * 128)
        for tg in range(2):
            pb = prr[0] % 4; prr[0] += 1
            proj_own(sl, tg, pb)
            act(gp[:, j, tg * 512:(tg + 1) * 512], PS[pb][:, :], AF.Silu, [PSB[pb]], [GP[j][tg]])
    for gi in range(4):
        dma("sp", invc, invc_d[gi], [], [INVC])
        for cc in range(2):
            j = gi * 2 + cc
            sl = load_wchunk(win_d, OFF_VP + j * 128)
            for tg in range(2):
                pb = prr[0] % 4; prr[0] += 1
                proj_own(sl, tg, pb)
                cp("act" if tg == 0 else "dve", vpx[:, cc, 8 + tg * 512:8 + (tg + 1) * 512], PS[pb][:, :], [PSB[pb]], [VPX[cc]])
            for k in range(16):
                mm(PS[4][:, 0:16], ws[sl][:, k, :], hhalo[:, k, :], k == 0, k == 15, [WS[sl]] + HHALO, [PSB[4]])
            tt("dve", vpx[:, cc, 0:8], PS[4][:, 0:8], mask_t[:, 0:8], ALU.mult, [PSB[4], MASK], [VPX[cc]])
            tt("dve", vpx[:, cc, 1032:1040], PS[4][:, 8:16], mask_t[:, 8:16], ALU.mult, [PSB[4], MASK], [VPX[cc]])
        lv = gi + 1
        tt("dve", s_a[:, :, 1:1040], vpx[:, :, 0:1039], vpx[:, :, 1:1040], ALU.add, VPX, [SA])
        cur, CUR, oth, OTH = s_a, SA, s_b, SBb
        if lv >= 2:
            tt("dve", oth[:, :, 2:1039], cur[:, :, 1:1038], cur[:, :, 3:1040], ALU.add, [CUR], [OTH])
            cur, CUR, oth, OTH = oth, OTH, cur, CUR
        if lv >= 3:
            tt("dve", oth[:, :, 4:1037], cur[:, :, 2:1035], cur[:, :, 6:1039], ALU.add, [CUR], [OTH])
            cur, CUR, oth, OTH = oth, OTH, cur, CUR
        if lv >= 4:
            tt("dve", oth[:, :, 8:1033], cur[:, :, 4:1029], cur[:, :, 12:1037], ALU.add, [CUR], [OTH])
            cur, CUR, oth, OTH = oth, OTH, cur, CUR
        for cc in range(2):
            tt("dve", oth[:, cc, 8:1032], cur[:, cc, 8:1032], invc, ALU.mult, [CUR, INVC], [OTH])
            tt("dve", pooled[:, cc, :], oth[:, cc, 8:1032], vpx[:, cc, 8:1032], ALU.subtract, [OTH, VPX[cc]], [POOLED])
        for jm in range(4):
            gmla_chunk(gi * 4 + jm)
        for dc in range(2):
            j = gi * 2 + dc
            for tg in range(2):
                pb = 5 + (prr[0] % 2); prr[0] += 1
                for kc in range(2):
                    mm(PS[pb][:, :], pw[:, gi, kc, dc * 128:(dc + 1) * 128], pooled[:, kc, tg * 512:(tg + 1) * 512], kc == 0, kc == 1, [PW, POOLED], [PSB[pb]])
                stt("dve", gp[:, j, tg * 512:(tg + 1) * 512], PS[pb][:, :], psc_col[:, j:j + 1], gp[:, j, tg * 512:(tg + 1) * 512],
                    ALU.mult, ALU.mult, [PSB[pb], PSC, GP[j][tg]], [GP[j][tg]])
    if debug == 2:
        dump("sg", sg, [128, 16, OWN], BF16, [b for r in SG for b in r])
        dump("gp", gp, [128, 8, OWN], BF16, [b for r in GP for b in r])
    checkpoint(2)
    S.barrier()
    A.top = mark1b

    mark2 = A.top
    wkv = [A.alloc([4, 256], BF16) for _ in range(2)]; WKV = [Buf(), Buf()]
    wq = [A.alloc([4, 256], BF16) for _ in range(2)]; WQ = [Buf(), Buf()]
    KT = [A.alloc([S_TOK], BF16) for _ in range(2)]; KTB = [[Buf() for _ in range(8)] for _ in range(2)]
    VV = [A.alloc([32, 128], BF16) for _ in range(2)]; VVB = [[Buf() for _ in range(32)] for _ in range(2)]
    QN = [A.alloc([OWN], BF16) for _ in range(2)]; QNB = [[Buf(), Buf()] for _ in range(2)]
    QR = [A.alloc([OWN], BF16) for _ in range(2)]; QRB = [[Buf(), Buf()] for _ in range(2)]
    QRU = [Buf(), Buf()]
    for i_ in range(2):
        S.op("pool", (lambda i_: (lambda e: e.memset(QR[i_][64:128, :], 0.0)))(i_), [], [QRU[i_]])
    NPT = 3
    PT = [A.alloc([1024], BF16) for _ in range(NPT)]; PTB = [Buf() for _ in range(NPT)]
    s1 = [A.alloc([512], BF16) for _ in range(2)]; S1 = [Buf(), Buf()]
    ssum = [A.alloc([512], BF16) for _ in range(2)]; SSUM = [Buf(), Buf()]
    rden = A.alloc([512], F32); RDEN = Buf()
    otmp = A.alloc([512], F32); OTMP = Buf()
    q1 = A.alloc([512], F32, parts=64); q2 = A.alloc([512], F32, parts=64); Q1 = Buf(); Q2 = Buf()
    xr = [0]
    pti = [0]

    def xbank():
        b = 6 + (xr[0] % 2)
        xr[0] += 1
        return b
    LA = 3

    def load_head_w(h):
        hs = h % 2
        dma("pool", wkv[hs], wukv_d[:, h * 256:(h + 1) * 256].rearrange("(k p) n -> p k n", p=128), [], [WKV[hs]])
        dma("pool", wq[hs][:, :, 0:192], wuq_d[:, h * 192:(h + 1) * 192].rearrange("(k p) n -> p k n", p=128), [], [WQ[hs]])
        dma("pool", wq[hs][:, :, 192:224], wuq_d[:, h * 192 + 160:h * 192 + 192].rearrange("(k p) n -> p k n", p=128), [], [WQ[hs]])
        dma("pool", wq[hs][:, :, 224:256], wuq_d[:, h * 192 + 128:h * 192 + 160].rearrange("(k p) n -> p k n", p=128), [], [WQ[hs]])

    def expansion_tasks(h):
        hs = h % 2
        tasks = []

        def k_task(tg):
            pb = xbank()
            for c in range(4):
                mm(PS[pb], wkv[hs][:, c, 0:128], ckvT[:, c, tg * 512:(tg + 1) * 512], c == 0, c == 3, [WKV[hs], CKVT[c][tg]], [PSB[pb]])
            cp("act" if tg % 2 == 0 else "dve", KT[hs][:, tg * 512:(tg + 1) * 512], PS[pb], [PSB[pb]], [KTB[hs][tg]])

        def v_task(j4):
            pb = xbank()
            for jj in range(4):
                j = j4 * 4 + jj
                for c in range(4):
                    mm(PS[pb][:, jj * 128:(jj + 1) * 128], ckvT[:, c, j * 128:(j + 1) * 128], wkv[hs][:, c, 128:256], c == 0, c == 3,
                       [WKV[hs], CKVT[c][j // 4]], [PSB[pb]])
            cp("dve" if j4 % 2 == 0 else "act", VV[hs][:, j4 * 4:(j4 + 1) * 4, :], PS[pb].rearrange("p (a b) -> p a b", a=4),
               [PSB[pb]], [VVB[hs][j4 * 4 + jj] for jj in range(4)])

        def qn_task(qg):
            pb = xbank()
            for c in range(4):
                mm(PS[pb], wq[hs][:, c, 0:128], cqT[:, c, qg * 512:(qg + 1) * 512], c == 0, c == 3, [WQ[hs], CQT[c][qg]], [PSB[pb]])
            cp("act", QN[hs][:, qg * 512:(qg + 1) * 512], PS[pb], [PSB[pb]], [QNB[hs][qg]])

        def qr_task(qg):
            pa = xbank()
            for c in range(4):
                mm(PS[pa], wq[hs][:, c, 128:256], cqT[:, c, qg * 512:(qg + 1) * 512], c == 0, c == 3, [WQ[hs], CQT[c][qg]], [PSB[pa]])
            tt("dve", q1, PS[pa][0:64, :], qcos[0:64, qg * 512:(qg + 1) * 512], ALU.mult, [PSB[pa], QCOS[qg]], [Q1])
            tt("dve", q2, PS[pa][64:128, :], qsin[64:128, qg * 512:(qg + 1) * 512], ALU.mult, [PSB[pa], QSIN[qg]], [Q2])
            tt("dve", QR[hs][0:64, qg * 512:(qg + 1) * 512], q1, q2, ALU.add, [Q1, Q2], [QRB[hs][qg]])

        for tg in range(8):
            tasks.append((lambda tg=tg: k_task(tg)))
        for j4 in range(8):
            tasks.append((lambda j4=j4: v_task(j4)))
        for qg in range(2):
            tasks.append((lambda qg=qg: qn_task(qg)))
            tasks.append((lambda qg=qg: qr_task(qg)))
        return tasks

    load_head_w(0)
    for tsk in expansion_tasks(0):
        tsk()
    for h in range(NH):
        hs = h % 2
        nxt = expansion_tasks(h + 1) if h + 1 < NH else []
        if h + 1 < NH:
            load_head_w(h + 1)
        for qg in range(2):
            qs = slice(qg * 512, (qg + 1) * 512)

            def s_pair(j):
                for u in range(2):
                    kc = 2 * j + u
                    sb_ = (j % 2) * 2 + u
                    mm(PS[sb_], KT[hs][:, kc * 128:(kc + 1) * 128], QN[hs][:, qs], True, False, [KTB[hs][kc // 4], QNB[hs][qg]], [PSB[sb_]])
                    mm(PS[sb_], krT[:, kc * 128:(kc + 1) * 128], QR[hs][:, qs], False, True, [KRT[kc // 4], KRU, QRB[hs][qg], QRU[hs]], [PSB[sb_]])
                b0 = (j % 2) * 2
                p = j % NPT
                act(PT[p], PSALL[:, b0 * 512:(b0 + 2) * 512], AF.Exp, [PSB[b0], PSB[b0 + 1]], [PTB[p]], scale=SM_SCALE)

            def pv_pair(j):
                p = j % NPT
                for u in range(2):
                    kc = 2 * j + u
                    mm(PS[4], VV[hs][:, kc, :], PT[p][:, u * 512:(u + 1) * 512], kc == 0, kc == 31, [VVB[hs][kc], PTB[p]], [PSB[4]])
                tt("dve", s1[j % 2], PT[p][:, 0:512], PT[p][:, 512:1024], ALU.add, [PTB[p]], [S1[j % 2]])
                if j % 2 == 1:
                    g4 = j // 2
                    tt("dve", ssum[g4 % 2], s1[0], s1[1], ALU.add, [S1[0], S1[1]], [SSUM[g4 % 2]])
                    if g4 % 2 == 1:
                        tt("dve", ssum[1], ssum[0], ssum[1], ALU.add, [SSUM[0], SSUM[1]], [SSUM[1]])

            def den_mm(g8):
                mm(PS[5], ones_bf, ssum[1], g8 == 0, g8 == 3, [ONES, SSUM[1]], [PSB[5]])

            issued = 0
            for j in range(16):
                while issued < min(16, j + 2):
                    s_pair(issued)
                    issued += 1
                pv_pair(j)
                if j % 4 == 0 and j >= 4:
                    den_mm(j // 4 - 1)
                if nxt and (qg * 16 + j) >= 2:
                    nxt.pop(0)()
            den_mm(3)
            if qg == 1:
                while nxt:
                    nxt.pop(0)()
            cp("dve", otmp, PS[4], [PSB[4]], [OTMP])
            S.op("dve", lambda e: e.reciprocal(out=rden, in_=PS[5]), [PSB[5]], [RDEN])
            tt("dve", otmp, otmp, rden, ALU.mult, [OTMP, RDEN], [OTMP])
            tt("dve", sg[:, h, qs], otmp, sg[:, h, qs], ALU.mult, [OTMP, SG[h][qg]], [SG[h][qg]])
    if debug == 3:
        dump("attn", sg, [128, 16, OWN], BF16, [b for r in SG for b in r])
    checkpoint(3)
    S.barrier()
    A.top = mark2

    yT = A.alloc_at(ckv_off, [16, OWN], BF16); YT = [[Buf(), Buf()] for _ in range(16)]
    mark3 = A.top
    wsA = [A.alloc([16, 128], BF16) for _ in range(2)]; WSA = [Buf(), Buf()]
    wsB = [A.alloc([8, 128], BF16) for _ in range(2)]; WSB = [Buf(), Buf()]
    wsC = [A.alloc([16, 128], BF16) for _ in range(2)]; WSC = [Buf(), Buf()]
    wsD = [A.alloc([16, 128], BF16) for _ in range(2)]; WSD = [Buf(), Buf()]
    sm1 = A.alloc([512], F32); sm2 = A.alloc([512], F32); SM1 = Buf(); SM2 = Buf()
    y1 = A.alloc([512], F32); y2 = A.alloc([512], F32); Y1 = Buf(); Y2 = Buf()
    gate_b = A.alloc_at(kr_off, [D], F32); GATEB = [Buf() for _ in range(4)]
    gate_row = A.alloc_at(kr_off + 8192, [D], F32); GROW = Buf()
    adabg_b = A.alloc_at(kr_off + 16384, [D], F32); ADABG = Buf()
    adaw_s2 = [A.alloc([16, 256], BF16) for _ in range(2)]; ADAWS2 = [Buf(), Buf()]
    dma("sp", adabg_b, adabg_d[0, :].partition_broadcast(128), [], [ADABG])
    it = 0
    for dc in range(16):
        s2 = dc % 2
        dma("pool", wsA[s2], womla_d[:, dc * 128:(dc + 1) * 128].rearrange("(k p) n -> p k n", p=128), [], [WSA[s2]])
        dma("pool", wsB[s2], wopool_d[:, dc * 128:(dc + 1) * 128].rearrange("(k p) n -> p k n", p=128), [], [WSB[s2]])
        dma("pool", wsC[s2], win_d[:, OFF_MM + dc * 128:OFF_MM + (dc + 1) * 128].rearrange("(k p) n -> p k n", p=128), [], [WSC[s2]])
        dma("pool", wsD[s2], win_d[:, OFF_MP + dc * 128:OFF_MP + (dc + 1) * 128].rearrange("(k p) n -> p k n", p=128), [], [WSD[s2]])
        if 1 <= dc <= 8:
            ada_load(4096, dc - 1, adaw_s2, ADAWS2, gw=256)
        for tg in range(2):
            bA, bB, bC, bD = (0, 1, 2, 3) if it % 2 == 0 else (4, 5, 6, 3)
            it += 1
            tsl = slice(tg * 512, (tg + 1) * 512)
            for k in range(16):
                mm(PS[bA], wsA[s2][:, k, :], sg[:, k, tsl], k == 0, k == 15, [WSA[s2], SG[k][tg]], [PSB[bA]])
            for k in range(8):
                mm(PS[bB], wsB[s2][:, k, :], gp[:, k, tsl], k == 0, k == 7, [WSB[s2], GP[k][tg]], [PSB[bB]])
            for k in range(16):
                mm(PS[bC], wsC[s2][:, k, :], hT_own[:, k, tsl], k == 0, k == 15, [WSC[s2], HT_OWN[k][tg]], [PSB[bC]])
            for k in range(16):
                mm(PS[bD], wsD[s2][:, k, :], hT_own[:, k, tsl], k == 0, k == 15, [WSD[s2], HT_OWN[k][tg]], [PSB[bD]])
            act(sm1, PS[bC], AF.Sigmoid, [PSB[bC]], [SM1])
            act(sm2, PS[bD], AF.Sigmoid, [PSB[bD]], [SM2])
            tt("dve", y1, PS[bA], sm1, ALU.mult, [PSB[bA], SM1], [Y1])
            tt("dve", y2, PS[bB], sm2, ALU.mult, [PSB[bB], SM2], [Y2])
            tt("dve", yT[:, dc, tsl], y1, y2, ALU.add, [Y1, Y2], [YT[dc][tg]])
        if 2 <= dc <= 9:
            ada_mm(dc - 2, gate_row, GROW, adaw_s2, ADAWS2, gw=256, bank=7)
    for cg in range(4):
        mm(PS[7], ones_f[0:1, 0:128], gate_row[0:1, cg * 512:(cg + 1) * 512], True, True, [ONESF, GROW], [PSB[7]])
        tt("dve", gate_b[:, cg * 512:(cg + 1) * 512], PS[7], adabg_b[:, cg * 512:(cg + 1) * 512], ALU.add, [PSB[7], ADABG], [GATEB[cg]])
    if debug == 4:
        dump("yT", yT, [128, 16, OWN], BF16, [b for r in YT for b in r])
    checkpoint(4)
    S.barrier()
    A.top = mark3

    A.top = after_ckv
    fg_b = A.alloc([D], F32); FGB = Buf()
    xsl = [A.alloc([512], F32) for _ in range(4)]; XSL = [Buf() for _ in range(4)]
    junk2 = A.alloc([D], BF16)
    rms2 = A.alloc([8], F32); RMS2 = [Buf() for _ in range(8)]
    rstd2 = A.alloc([8], F32); RSTD2 = [Buf() for _ in range(8)]
    assert A.top <= kr_off
    A.top = kr_off + 8192
    wo = A.alloc([16, D], BF16); WO = [Buf() for _ in range(4)]
    rbuf = [A.alloc([D], F32) for _ in range(8)]; RB = [[Buf() for _ in range(4)] for _ in range(8)]
    OUTB = [Buf() for _ in range(8)]
    for cg in range(4):
        for q in range(4):
            dma("pool", wo[:, 4 * q:4 * q + 4, cg * 512:(cg + 1) * 512],
                wout_d[512 * q:512 * (q + 1), cg * 512:(cg + 1) * 512].rearrange("(k p) n -> p k n", p=128),
                [WO[cg - 1]] if cg > 0 else [], [WO[cg]])
    dma("sp", fg_b, fg_d[0, :].partition_broadcast(128), [], [FGB])
    pr = 0
    xi = 0
    pend3b = []
    order = [(cg, tt_i) for cg in (0, 1) for tt_i in range(8)] + [(cg, tt_i) for tt_i in range(8) for cg in (2, 3)]
    for (cg, tt_i) in order:
        cs = slice(cg * 512, (cg + 1) * 512)
        if True:
            pb = pr % 6
            pr += 1
            xs_ = xi % 4
            xi += 1
            dma("sp", xsl[xs_], x_d[tt_i * 128:(tt_i + 1) * 128, cs], [], [XSL[xs_]])
            for k in range(16):
                mm(PS[pb], yT[:, k, tt_i * 128:(tt_i + 1) * 128], wo[:, k, cs], k == 0, k == 15, [YT[k][tt_i // 4], WO[cg]], [PSB[pb]])
            tt("dve", rbuf[tt_i][:, cs], PS[pb], gate_b[:, cs], ALU.mult, [PSB[pb], GATEB[cg]], [RB[tt_i][cg]])
            tt("dve", rbuf[tt_i][:, cs], rbuf[tt_i][:, cs], xsl[xs_], ALU.add, [RB[tt_i][cg], XSL[xs_]], [RB[tt_i][cg]])
            if cg == 3:
                ssq, SSQ = newstat()
                i8 = tt_i
                act(junk2, rbuf[tt_i], AF.Square, RB[tt_i], [SSQ], accum_out=ssq)
                act(rms2[:, i8:i8 + 1], ssq, AF.Sqrt, [SSQ, EPSB], [RMS2[i8]], bias=eps_col, scale=1.0 / D)
                def fin(tt_i=tt_i, i8=i8):
                    S.op("dve", lambda e: e.reciprocal(out=rstd2[:, i8:i8 + 1], in_=rms2[:, i8:i8 + 1]), [RMS2[i8]], [RSTD2[i8]])
                    act(rbuf[tt_i], rbuf[tt_i], AF.Identity, RB[tt_i] + [RSTD2[i8]], RB[tt_i], scale=rstd2[:, i8:i8 + 1])
                    tt("pool", rbuf[tt_i], rbuf[tt_i], fg_b, ALU.mult, RB[tt_i] + [FGB], RB[tt_i])
                    dma("pool", out_d[tt_i * 128:(tt_i + 1) * 128, :], rbuf[tt_i], RB[tt_i], [], key=OUTB[tt_i])
                if pend3b:
                    pend3b.pop(0)()
                pend3b.append(fin)
    while pend3b:
        pend3b.pop(0)()
    S.op("sp", lambda e: e.nop(), [], [b_ for r_ in RB for b_ in r_])

    build(nc, S)
    es.close()


_NC_CACHE = {}


def _prep_inputs(x, c, positions, ada_w, ada_b, norm_g, w_in, q_norm_g, w_uq, kv_norm_g,
                 w_ukv, w_o_mla, pool_w, pool_scale, w_o_pool, w_out, final_g):
    f32 = np.float32
    x = np.asarray(x, f32); c = np.asarray(c, f32); positions = np.asarray(positions, np.int32)

    def col(v, n):
        return np.ascontiguousarray(np.asarray(v, f32).reshape(n, 128).T)

    ada_b0 = np.asarray(ada_b, f32)[0]
    inv_freq = (1.0 / (10000.0 ** (np.arange(0, 64, 2, dtype=np.float32) / np.float32(64)))).astype(f32)
    freq_col = np.concatenate([-inv_freq, inv_freq, -inv_freq, inv_freq]).reshape(128, 1).astype(f32)
    shared = {
        "ada_w": np.ascontiguousarray(np.asarray(ada_w, f32)[0]),
        "ada_b_col": np.ascontiguousarray(np.concatenate([col(ada_b0[0:D], 16), col(ada_b0[D:2 * D], 16)], axis=1)),
        "ada_b_gate": np.ascontiguousarray(ada_b0[2 * D:3 * D].reshape(1, D)),
        "norm_g_col": col(np.asarray(norm_g, f32)[0], 16),
        "q_norm_g_col": col(np.asarray(q_norm_g, f32)[0], 4),
        "kv_norm_g_col": col(np.asarray(kv_norm_g, f32)[0], 4),
        "pool_scale_col": col(np.asarray(pool_scale, f32)[0], 8),
        "final_g": np.ascontiguousarray(np.asarray(final_g, f32).reshape(1, D)),
        "w_in": np.ascontiguousarray(np.asarray(w_in, f32)[0]),
        "w_uq": np.ascontiguousarray(np.asarray(w_uq, f32)[0]),
        "w_ukv": np.ascontiguousarray(np.asarray(w_ukv, f32)[0]),
        "w_o_mla": np.ascontiguousarray(np.asarray(w_o_mla, f32)[0]),
        "pool_w": np.ascontiguousarray(np.asarray(pool_w, f32)[0]),
        "w_o_pool": np.ascontiguousarray(np.asarray(w_o_pool, f32)[0]),
        "w_out": np.ascontiguousarray(np.asarray(w_out, f32)[0]),
        "ident": np.eye(128).astype(ml_dtypes.bfloat16),
        "freq_col": freq_col,
    }
    in_maps = []
    wins = (2, 4, 8, 16)
    for core in range(8):
        b, qi = core // 4, core % 4
        start = qi * OWN
        m = dict(shared)
        m["x_r"] = np.ascontiguousarray(np.roll(x[b], -start, axis=0))
        m["pos_r"] = np.ascontiguousarray(np.roll(positions[b], -start).reshape(1, S_TOK))
        m["c_col"] = col(c[b], 16)
        t = start + np.arange(OWN)
        invc = np.zeros((4, 128, OWN), f32)
        for gi, w in enumerate(wins):
            lo = np.clip(t - w // 2, 0, S_TOK)
            hi = np.clip(t + w - w // 2, 0, S_TOK)
            invc[gi] = (1.0 / (hi - lo).astype(f32))[None, :]
        m["pool_invc"] = invc
        halo_tok = np.concatenate([start - 8 + np.arange(8), start + OWN + np.arange(8)])
        valid = ((halo_tok >= 0) & (halo_tok < S_TOK)).astype(f32)
        m["pool_mask"] = np.ascontiguousarray(np.broadcast_to(valid[None, :], (128, 16))).astype(f32)
        in_maps.append(m)
    return in_maps


def kernel(x, c, positions, ada_w, ada_b, norm_g, w_in, q_norm_g, w_uq, kv_norm_g,
           w_ukv, w_o_mla, pool_w, pool_scale, w_o_pool, w_out, final_g):
    in_maps = _prep_inputs(x, c, positions, ada_w, ada_b, norm_g, w_in, q_norm_g, w_uq, kv_norm_g,
                           w_ukv, w_o_mla, pool_w, pool_scale, w_o_pool, w_out, final_g)
    if "nc" not in _NC_CACHE:
        _NC_CACHE["nc"] = build_program()
    nc = _NC_CACHE["nc"]
    res = run_bass_kernel_spmd(nc, in_maps, core_ids=list(range(8)))
    out = np.zeros((2, S_TOK, D), np.float32)
    for core in range(8):
        b, qi = core // 4, core % 4
        out[b, qi * OWN:(qi + 1) * OWN, :] = res.results[core]["out"]
    return out
```
